# Optimizing a Trainium2 kernel written in Bass

```python
import jax, jax.numpy as jnp
from jax import lax
import numpy as np

D_MODEL = 4096
BATCH = 2
SEQ = 8192
DEPTH = 1

PLE_DIM = 256
POOL_WIDTH = D_MODEL // 2
POOL_WINDOWS = (2, 4, 8, 16)
POOL_GROUPS = len(POOL_WINDOWS)
POOL_GROUP_W = POOL_WIDTH // POOL_GROUPS
HEAD_DIM = 128
ATTN_WIDTH = D_MODEL // 2
N_HEADS = ATTN_WIDTH // HEAD_DIM
MOBA_BLOCK = 256
MOBA_TOPK = 3
Q_CHUNK = 32
RMS_EPS = 1e-6
IN_WIDTH = 2 * POOL_WIDTH + 4 * ATTN_WIDTH + 2 * D_MODEL

kernel_name = "hybrid_pool_moba_gated_merge"


def rms_norm(x, gain):
    xf = x.astype(jnp.float32)
    y = xf * lax.rsqrt(jnp.mean(xf * xf, axis=-1, keepdims=True) + RMS_EPS)
    return (y * gain.astype(jnp.float32)).astype(x.dtype)


def multiscale_pool(u, group_w, scale):
    B_, S_, _ = u.shape
    uf = u.astype(jnp.float32)
    c = lax.cumsum(uf, axis=1)
    wmax = max(POOL_WINDOWS)
    cpad = jnp.pad(c, ((0, 0), (wmax, 0), (0, 0)))
    t = jnp.arange(S_)[None, :, None]
    outs = []
    for g, w in enumerate(POOL_WINDOWS):
        lo, hi = g * POOL_GROUP_W, (g + 1) * POOL_GROUP_W
        prev = cpad[:, wmax - w: wmax - w + S_, lo:hi]
        count = jnp.minimum(t + 1, w).astype(jnp.float32)
        outs.append((c[..., lo:hi] - prev) / count)
    pooled = jnp.stack(outs, axis=2)
    diff = (pooled - uf.reshape(B_, S_, POOL_GROUPS, POOL_GROUP_W)).astype(u.dtype)
    mixed = jnp.einsum('bsgc,gcd->bsgd', diff, group_w).reshape(B_, S_, POOL_WIDTH)
    return mixed * scale


def moba_attention(q, k, v):
    B_, S_, H_, Dh = q.shape
    f32 = jnp.float32
    nb = -(-S_ // MOBA_BLOCK)
    s_pad = nb * MOBA_BLOCK
    pad = ((0, 0), (0, s_pad - S_), (0, 0), (0, 0))
    qf = jnp.pad(q, pad).astype(f32) * (Dh ** -0.5)
    kb = jnp.pad(k, pad).astype(f32).reshape(B_, nb, MOBA_BLOCK, H_, Dh).transpose(0, 3, 1, 2, 4)
    vb = jnp.pad(v, pad).astype(f32).reshape(B_, nb, MOBA_BLOCK, H_, Dh).transpose(0, 3, 1, 2, 4)
    kmean = jnp.mean(kb, axis=3)
    n_chunks = s_pad // Q_CHUNK
    qc = qf.reshape(B_, n_chunks, Q_CHUNK, H_, Dh).transpose(1, 0, 3, 2, 4)
    n_sel = min(MOBA_TOPK, nb)
    blk_ids = jnp.arange(nb)
    gather_blocks = jax.vmap(jax.vmap(lambda blocks, idx: blocks[idx]))

    def one_chunk(args):
        q_c, ci = args
        q_pos = ci * Q_CHUNK + jnp.arange(Q_CHUNK)
        j = (ci * Q_CHUNK) // MOBA_BLOCK
        gate = jnp.einsum('bhqd,bhnd->bhqn', q_c, kmean)
        gate = jnp.where(blk_ids < j, gate, -jnp.inf)
        _, sel = lax.top_k(gate, n_sel)
        sel_ok = sel < j
        k_sel = gather_blocks(kb, sel)
        v_sel = gather_blocks(vb, sel)
        s_sel = jnp.einsum('bhqd,bhqnmd->bhqnm', q_c, k_sel)
        s_sel = jnp.where(sel_ok[..., None], s_sel, -jnp.inf)
        k_own = lax.dynamic_index_in_dim(kb, j, axis=2, keepdims=False)
        v_own = lax.dynamic_index_in_dim(vb, j, axis=2, keepdims=False)
        s_own = jnp.einsum('bhqd,bhmd->bhqm', q_c, k_own)
        k_pos = j * MOBA_BLOCK + jnp.arange(MOBA_BLOCK)
        s_own = jnp.where(k_pos[None, :] <= q_pos[:, None], s_own, -jnp.inf)
        s_all = jnp.concatenate(
            [s_sel.reshape(B_, H_, Q_CHUNK, n_sel * MOBA_BLOCK), s_own], axis=-1)
        prob = jax.nn.softmax(s_all, axis=-1)
        p_sel = prob[..., :n_sel * MOBA_BLOCK].reshape(B_, H_, Q_CHUNK, n_sel, MOBA_BLOCK)
        p_own = prob[..., n_sel * MOBA_BLOCK:]
        return (jnp.einsum('bhqnm,bhqnmd->bhqd', p_sel, v_sel)
                + jnp.einsum('bhqm,bhmd->bhqd', p_own, v_own))

    out = lax.map(one_chunk, (qc, jnp.arange(n_chunks)))
    out = out.transpose(1, 0, 3, 2, 4).reshape(B_, s_pad, H_ * Dh)[:, :S_]
    return out.astype(q.dtype)


def setup_inputs(seed: int = 0) -> dict:
    key = jax.random.key(seed)
    ks = jax.random.split(key, 14)
    f32 = jnp.float32
    nrm = lambda k, shape, fan_in: jax.random.normal(k, shape, f32) * (fan_in ** -0.5)
    return {
        "x": jax.random.normal(ks[0], (BATCH, SEQ, D_MODEL), f32),
        "p": jax.random.normal(ks[1], (DEPTH, BATCH, SEQ, PLE_DIM), f32),
        "norm_pre": 1.0 + 0.02 * jax.random.normal(ks[2], (DEPTH, D_MODEL), f32),
        "w_in": nrm(ks[3], (DEPTH, D_MODEL, IN_WIDTH), D_MODEL),
        "pool_group_w": nrm(ks[4], (DEPTH, POOL_GROUPS, POOL_GROUP_W, POOL_GROUP_W), POOL_GROUP_W),
        "pool_scale": 1.0 + 0.1 * jax.random.normal(ks[5], (DEPTH, POOL_WIDTH), f32),
        "w_pool_out": nrm(ks[6], (DEPTH, POOL_WIDTH, D_MODEL), POOL_WIDTH),
        "w_attn_out": nrm(ks[7], (DEPTH, ATTN_WIDTH, D_MODEL), ATTN_WIDTH),
        "w_out": nrm(ks[8], (DEPTH, D_MODEL, D_MODEL), D_MODEL),
        "norm_post": 1.0 + 0.02 * jax.random.normal(ks[9], (DEPTH, D_MODEL), f32),
        "w_ple_proj": nrm(ks[10], (DEPTH, PLE_DIM, D_MODEL), PLE_DIM),
        "w_ple_gate": nrm(ks[11], (DEPTH, D_MODEL, D_MODEL), D_MODEL),
    }


def reference(x, p, norm_pre, w_in, pool_group_w, pool_scale, w_pool_out, w_attn_out,
              w_out, norm_post, w_ple_proj, w_ple_gate):
    B_, S_, _ = x.shape
    cuts = [POOL_WIDTH, POOL_WIDTH, ATTN_WIDTH, ATTN_WIDTH, ATTN_WIDTH, ATTN_WIDTH, D_MODEL]
    split_points = [int(c) for c in np.cumsum(cuts)]
    for i in range(DEPTH):
        h = rms_norm(x, norm_pre[i])
        z = h @ w_in[i]
        u_pool, g_pool, q, k, v, g_attn, m_pool, m_attn = jnp.split(z, split_points, axis=-1)
        y_pool = multiscale_pool(u_pool, pool_group_w[i], pool_scale[i]) * jax.nn.silu(g_pool)
        y_attn = moba_attention(q.reshape(B_, S_, N_HEADS, HEAD_DIM),
                                k.reshape(B_, S_, N_HEADS, HEAD_DIM),
                                v.reshape(B_, S_, N_HEADS, HEAD_DIM)) * jax.nn.silu(g_attn)
        merged = (jax.nn.sigmoid(m_pool) * (y_pool @ w_pool_out[i])
                  + jax.nn.sigmoid(m_attn) * (y_attn @ w_attn_out[i]))
        x = x + rms_norm(merged @ w_out[i], norm_post[i])
        x = x + jax.nn.sigmoid(x @ w_ple_gate[i]) * (p[i] @ w_ple_proj[i])
    return x
```

```python
import numpy as np
import concourse.bass as bass
import concourse.mybir as mybir
from concourse.bass_utils import run_bass_kernel_spmd

F32 = mybir.dt.float32
BF16 = mybir.dt.bfloat16
AF = mybir.ActivationFunctionType
ALU = mybir.AluOpType
AX = mybir.AxisListType

D = 4096
S = 8192
NEG = -30000.0
EPS = 1e-6
WINS = (2, 4, 8, 16)
DEBUG_OUTS = ()


class Sched:
    def __init__(self, nc, prog):
        self.nc = nc
        self.eng = {"pe": nc.tensor, "act": nc.scalar, "dve": nc.vector, "pool": nc.gpsimd, "sp": nc.sync}
        self.prog = prog
        self.cnt = {k: 0 for k in prog}
        self.waited = {k: {} for k in self.eng}
        self.dcnt = {}

    def wait(self, e, toks):
        w = self.waited[e]
        for t in toks:
            if t is None:
                continue
            name, sem, val = t
            if w.get(name, 0) < val:
                self.eng[e].wait_ge(sem, val)
                w[name] = val

    def op(self, e, fn, deps=(), signal=True):
        self.wait(e, deps)
        ins = fn(self.eng[e])
        if signal:
            self.cnt[e] += 1
            ins.then_inc(self.prog[e][1], 1)
            return (self.prog[e][0], self.prog[e][1], self.cnt[e])
        return None

    def dma(self, q, sem, out, in_, deps=()):
        self.wait(q, deps)
        ins = self.eng[q].dma_start(out=out, in_=in_)
        ins.then_inc(sem[1], 16)
        self.dcnt[sem[0]] = self.dcnt.get(sem[0], 0) + 16
        return (sem[0], sem[1], self.dcnt[sem[0]])


def build_program():
    nc = bass.Bass("TRN2", target_bir_lowering=False)

    def din(name, shape, dt=F32):
        return nc.dram_tensor(name, list(shape), dt, kind="ExternalInput").ap()

    def scratch(name, shape, dt):
        kind = "ExternalOutput" if name in DEBUG_OUTS else "Internal"
        return nc.dram_tensor(name, list(shape), dt, kind=kind).ap()

    xs = din("xs", [S, D])
    xo = din("xo", [2, 1152, D])
    pT = din("pT", [256, 2048])
    w_in = din("w_in", [D, 20480])
    gw = din("gw", [4, 512, 512])
    wpo = din("wpo", [2048, D])
    wao = din("wao", [2048, D])
    wo = din("wo", [D, D])
    wg = din("wg", [D, D])
    wp = din("wp", [256, D])
    g1rep = din("g1rep", [128, D])
    g2rep = din("g2rep", [128, D])
    pscale = din("pscale", [128, 16])
    ident_in = din("ident", [128, 128])
    ones_in = din("ones", [128, 128])
    onehot_in = din("onehot", [128, 4096])
    cmask_in = din("cmask", [128, 16, 512])
    pastmask_in = din("pastmask", [128, 8, 32])
    pastvalid_in = din("pastvalid", [128, 8, 32])
    ownbias_in = din("ownbias", [128, 8, 32])
    rc0_in = din("rc0", [128, 4, 16])
    out_own = nc.dram_tensor("out_own", [16, 128, D], F32, kind="ExternalOutput").ap()

    hT_all = scratch("hT_all", [128, 32, S], BF16)
    kT_all = scratch("kT_all", [128, 16, S], BF16)
    v_all = scratch("v_all", [64, 128, 2048], BF16)
    qT_s = scratch("qT_s", [128, 16, 1024], BF16)
    sga_s = scratch("sga_s", [128, 16, 1024], BF16)
    smp_s = scratch("smp_s", [128, 32, 1024], BF16)
    sma_s = scratch("sma_s", [128, 32, 1024], BF16)
    O_s = scratch("O_s", [8, 128, D], F32)
    x1_s = scratch("x1_s", [8, 128, D], F32)

    w_in_v = w_in.rearrange("(kc p) n -> p kc n", p=128)
    gw_v = gw.rearrange("g (kc p) n -> p g kc n", p=128)
    wpo_v = wpo.rearrange("(kc p) n -> p kc n", p=128)
    wao_v = wao.rearrange("(kc p) n -> p kc n", p=128)
    wo_v = wo.rearrange("(kc p) n -> p kc n", p=128)
    wg_v = wg.rearrange("(kc p) n -> p kc n", p=128)
    wp_v = wp.rearrange("(kc p) n -> p kc n", p=128)
    pT_v = pT.rearrange("(kc p) n -> p kc n", p=128)
    v_all_v = v_all.rearrange("c p f -> p c f")

    from contextlib import ExitStack
    es = ExitStack()
    with es:
        def sb(name, shape, dt):
            return es.enter_context(nc.sbuf_tensor(name, list(shape), dt))

        semreg = {}

        def mksem(name):
            if name not in semreg:
                semreg[name] = (name, es.enter_context(nc.semaphore(name)))
            return semreg[name]

        prog = {k: mksem("prog_" + k) for k in ("pe", "act", "dve", "pool")}
        ps = [es.enter_context(nc.psum_tensor("ps%d" % i, [128, 512], F32)) for i in range(6)]
        pb = [es.enter_context(nc.psum_tensor("pb%d" % i, [128, 8, 128], BF16)) for i in range(2)]
        ident = sb("ident_sb", [128, 128], BF16)
        ones = sb("ones_sb", [128, 128], BF16)
        ksum = sb("ksum", [128, 512], F32)
        kmean = sb("kmean", [128, 512], BF16)
        stat = sb("stat", [128, 64], F32)
        rinv = sb("rinv", [128, 64], F32)
        pscale_sb = sb("pscale_sb", [128, 16], F32)
        pastmask = sb("pastmask_sb", [128, 8, 32], F32)
        pastvalid = sb("pastvalid_sb", [128, 8, 32], F32)
        ownbias = sb("ownbias_sb", [128, 8, 32], F32)
        rc0 = sb("rc0_sb", [128, 4, 16], F32)

        block = es.enter_context(nc.Block())
        sc = Sched(nc, prog)
        op, dma = sc.op, sc.dma

        def barrier():
            toks = [(prog[k][0], prog[k][1], sc.cnt[k]) for k in prog if sc.cnt[k] > 0]
            toks += [(n, semreg[n][1], v) for n, v in sc.dcnt.items()]
            return toks

        sem_c = mksem("const_sp")
        sem_cp = mksem("const_pool")
        ctoks = []
        for dst, src in ((pscale_sb, pscale), (pastmask, pastmask_in), (pastvalid, pastvalid_in),
                         (ownbias, ownbias_in), (rc0, rc0_in)):
            ctoks.append(dma("sp", sem_c, dst[:], src))
        ctok_sp = ctoks[-1]
        t1 = dma("pool", sem_cp, ident[:], ident_in)
        t2 = dma("pool", sem_cp, ones[:], ones_in)
        ctok_pool = t2

        eps_t = sb("eps_t", [128, 1], F32)
        eps_tok = op("dve", lambda e: e.memset(eps_t[:], EPS))

        class NT:
            def __init__(self, stack, tag, grep_dram, init):
                self.g1 = stack.enter_context(nc.sbuf_tensor("g1" + tag, [128, D], F32))
                self.xt = [stack.enter_context(nc.sbuf_tensor("xt%s%d" % (tag, i), [128, D], F32)) for i in range(2)]
                self.xsb = [stack.enter_context(nc.sbuf_tensor("xsb%s%d" % (tag, i), [128, D], BF16)) for i in range(2)]
                self.sem_x = [mksem("ntx%d" % i) for i in range(2)]
                self.sem_g = mksem("ntg")
                self.g1tok = dma("sp", self.sem_g, self.g1[:], grep_dram, deps=init)
                self.xt_free = [list(init), list(init)]
                self.xsb_free = [list(init), list(init)]
                self.pb_free = [list(init) for _ in range(4)]
                self.d2 = {}

            def p1(self, t, src_ap):
                b = t % 2
                xt_ap, xsb_ap = self.xt[b][:], self.xsb[b][:]
                tok_x = dma("sp", self.sem_x[b], xt_ap, src_ap, deps=self.xt_free[b])
                a1 = op("dve", lambda e: e.scalar_tensor_tensor(out=xsb_ap, in0=xt_ap, scalar=1.0, in1=xt_ap,
                                                                op0=ALU.mult, op1=ALU.mult,
                                                                accum_out=stat[:, t:t + 1]),
                        deps=[tok_x, eps_tok] + self.xsb_free[b])
                a2 = op("act", lambda e: e.activation(out=stat[:, t:t + 1], in_=stat[:, t:t + 1],
                                                       func=AF.Sqrt, scale=1.0 / D, bias=eps_t[:, 0:1]), deps=[a1])
                d1 = op("dve", lambda e: e.reciprocal(out=rinv[:, t:t + 1], in_=stat[:, t:t + 1]), deps=[a2])
                d2 = op("dve", lambda e: e.scalar_tensor_tensor(out=xsb_ap, in0=xt_ap, scalar=rinv[:, t:t + 1],
                                                                in1=self.g1[:], op0=ALU.mult, op1=ALU.mult),
                        deps=[d1, a1, self.g1tok])
                self.xt_free[b] = [d2]
                self.d2[t] = d2

            def p2(self, t, evac):
                b = t % 2
                xsb_ap = self.xsb[b][:]
                d2 = self.d2.pop(t)
                tp = None
                for g8 in range(8):
                    sl = g8 % 2
                    bank = pb[sl]
                    hh = 0
                    for j in range(4):
                        kc = g8 * 4 + j
                        tp = op("pe", lambda e, kc=kc, j=j, bank=bank, hh=hh: e.transpose(
                            out=bank[:, hh + j, :], in_=xsb_ap[:, kc * 128:(kc + 1) * 128], identity=ident[:]),
                            deps=[d2, ctok_pool] + self.pb_free[sl], signal=(j == 3))
                    self.pb_free[sl] = evac(g8, bank[:, hh:hh + 4, :], tp)
                self.xsb_free[b] = [tp]

        s1 = ExitStack()
        with s1:
            def sb1(name, shape, dt):
                return s1.enter_context(nc.sbuf_tensor(name, list(shape), dt))
            wsb = [sb1("wsb%d" % i, [128, 32, 512], BF16) for i in range(2)]
            hsb = [sb1("hsb%d" % i, [128, 32, 512], BF16) for i in range(2)]
            kst = [sb1("kst%d" % i, [128, 512], BF16) for i in range(4)]
            sem_w = [mksem("s1w%d" % i) for i in range(2)]
            sem_hh = [mksem("s1h%d" % i) for i in range(2)]
            sem_ho = [mksem("s1o%d" % i) for i in range(2)]
            sem_k = [mksem("s1k%d" % i) for i in range(4)]
            w_free = [[], []]
            h_free = [[], []]
            kst_free = [[], [], [], []]
            bank_free = [[], [], [], []]
            KV0 = 6144
            wtoks = {}
            htoks = {}
            nk = [0]

            def load_w(wb):
                wtoks[wb] = dma("pool", sem_w[wb % 2], wsb[wb % 2][:],
                                w_in_v[:, :, KV0 + wb * 512:KV0 + (wb + 1) * 512], deps=w_free[wb % 2])

            def mm_group(wb, tt, H, hdeps, cs=(0, 1, 2, 3)):
                W = wsb[wb % 2]
                mm = None
                for c in cs:
                    bi = nk[0] % 4
                    bank = ps[bi]
                    for kc in range(32):
                        deps = [wtoks[wb]] + list(hdeps) + bank_free[bi] if kc == 0 else ()
                        if wb < 4:
                            mm = op("pe", lambda e, kc=kc, c=c: e.matmul(
                                bank[:, :], lhsT=W[:, kc, c * 128:(c + 1) * 128], rhs=H[:, kc, :],
                                start=(kc == 0), stop=(kc == 31)), deps=deps, signal=(kc == 31))
                        else:
                            mm = op("pe", lambda e, kc=kc, c=c: e.matmul(
                                bank[:, :], lhsT=H[:, kc, c * 128:(c + 1) * 128], rhs=W[:, kc, :],
                                start=(kc == 0), stop=(kc == 31)), deps=deps, signal=(kc == 31))
                    sl = nk[0] % 4
                    nk[0] += 1
                    if wb < 4:
                        h = wb * 4 + c
                        e1 = op("act", lambda e: e.activation(
                            out=kst[sl][:, 0:256], in_=bank[:, 0:256], func=AF.Copy,
                            accum_out=ksum[:, h * 32 + 2 * tt:h * 32 + 2 * tt + 1]),
                            deps=[mm] + kst_free[sl])
                        e2 = op("act", lambda e: e.activation(
                            out=kst[sl][:, 256:512], in_=bank[:, 256:512], func=AF.Copy,
                            accum_out=ksum[:, h * 32 + 2 * tt + 1:h * 32 + 2 * tt + 2]), deps=[mm])
                        dst = kT_all[:, h, tt * 512:(tt + 1) * 512]
                    else:
                        e2 = op("act", lambda e: e.copy(out=kst[sl][:], in_=bank[:, :]),
                                deps=[mm] + kst_free[sl])
                        dst = v_all[tt * 4 + c, :, (wb - 4) * 512:(wb - 3) * 512]
                    bank_free[bi] = [e2]
                    od = dma("act", sem_k[sl], dst, kst[sl][:], deps=[e2])
                    kst_free[sl] = [od]
                return mm

            load_w(0)
            load_w(1)
            s0 = ExitStack()
            with s0:
                nt = NT(s0, "a", g1rep, [])
                nt.p1(0, xs[0:128, :])
                pend = None
                for T in range(16):
                    hb = T % 2
                    evs = []
                    for tq in range(4):
                        t = 4 * T + tq
                        if t + 1 < 64:
                            nt.p1(t + 1, xs[(t + 1) * 128:(t + 2) * 128, :])

                        def evac(g8, bank, tp, hb=hb, tq=tq):
                            dst = hsb[hb][:, g8 * 4:(g8 + 1) * 4, tq * 128:(tq + 1) * 128]
                            ev = op("act", lambda e: e.copy(out=dst, in_=bank), deps=[tp] + h_free[hb])
                            evs.append(ev)
                            return [ev]
                        nt.p2(t, evac)
                        if pend is not None:
                            pT_, phb, pev, pod = pend
                            lastmm = mm_group(0, pT_, hsb[phb], pev, cs=(tq,))
                            if tq == 3:
                                h_free[phb] = [lastmm, pod]
                    od = dma("act", sem_ho[hb], hT_all[:, :, T * 512:(T + 1) * 512], hsb[hb][:], deps=evs[-1:])
                    if T == 0:
                        first_od = od
                    pend = (T, hb, evs[-1:], od)
                def load_h(it):
                    tt = it % 16
                    htoks[it] = dma("sp", sem_hh[it % 2], hsb[it % 2][:], hT_all[:, :, tt * 512:(tt + 1) * 512],
                                    deps=h_free[it % 2])

                pT_, phb, pev, pod = pend
                h_free[0] = h_free[0] + [first_od]
                load_h(16)
                lastmm = mm_group(0, pT_, hsb[phb], pev)
                h_free[phb] = [lastmm, pod]
                w_free[0] = [lastmm]
            pass0_done = barrier()
            h_free[1] = list(pass0_done)
            for wb in range(1, 8):
                if wb + 1 < 8:
                    load_w(wb + 1)
                for tt in range(16):
                    it = wb * 16 + tt
                    if it + 1 < 128:
                        load_h(it + 1)
                    lastmm = mm_group(wb, tt, hsb[it % 2], [htoks[it]])
                    h_free[it % 2] = [lastmm]
                w_free[wb % 2] = [lastmm]
            km_tok = op("dve", lambda e: e.tensor_scalar(out=kmean[:], in0=ksum[:], scalar1=1.0 / 256.0,
                                                         scalar2=None, op0=ALU.mult),
                        deps=barrier())

        prev_stage_toks = barrier()
        for hf in range(2):
            hs = ExitStack()
            with hs:
                def sbh(name, shape, dt):
                    return hs.enter_context(nc.sbuf_tensor(name, list(shape), dt))
                ypool = sbh("ypool%d" % hf, [128, 16, 1024], BF16)
                s2 = ExitStack()
                with s2:
                    def sb2(name, shape, dt):
                        return s2.enter_context(nc.sbuf_tensor(name, list(shape), dt))
                    prev_stage_toks = barrier()
                    hTo = sb2("hTo%d" % hf, [128, 32, 1152], BF16)
                    a_s = ExitStack()
                    with a_s:
                        nt = NT(a_s, "b%d" % hf, g1rep, prev_stage_toks)
                        hTo_toks = []
                        nt.p1(0, xo[hf, 0:128, :])
                        for t in range(9):
                            if t + 1 < 9:
                                nt.p1(t + 1, xo[hf, (t + 1) * 128:(t + 2) * 128, :])

                            def evac(g8, bank, tp, t=t):
                                dst = hTo[:, g8 * 4:(g8 + 1) * 4, t * 128:(t + 1) * 128]
                                ev = op("act", lambda e: e.copy(out=dst, in_=bank), deps=[tp] + list(prev_stage_toks))
                                hTo_toks.append(ev)
                                return [ev]
                            nt.p2(t, evac)
                        hTo_ready = hTo_toks[-1:]
                    s2a_done = barrier()
                    prev_stage_toks = s2a_done
                    w2 = [sb2("w2_%d_%d" % (hf, i), [128, 32, 256], BF16) for i in range(2)]
                    SG = sb2("SG%d" % hf, [128, 4, 1024], BF16)
                    diffT = sb2("diffT%d" % hf, [128, 4, 1024], BF16)
                    U = [sb2("U%d_%d" % (hf, i), [128, 528], F32) for i in range(2)]
                    TA = [sb2("TA%d_%d" % (hf, i), [128, 528], F32) for i in range(2)]
                    tmp16 = sb2("tmp16_%d" % hf, [128, 16], F32)
                    stg = [sb2("stg%d_%d" % (hf, i), [128, 512], BF16) for i in range(4)]
                    gwsb = sb2("gwsb%d" % hf, [128, 4, 4, 512], BF16)
                    sem_w2 = [mksem("s2w_%d" % i) for i in range(2)]
                    sem_stg = [mksem("s2s_%d" % i) for i in range(4)]
                    sem_gw = mksem("s2gw")
                    gwtok = dma("pool", sem_gw, gwsb[:], gw_v, deps=s2a_done)
                    blocks = []
                    for g in range(4):
                        blocks.append(("gp", 2048 + 512 * g, 0, g))
                        blocks.append(("gp", 2048 + 512 * g + 256, 2, g))
                        blocks.append(("u", 512 * g, 0, g))
                        blocks.append(("u", 512 * g + 256, 2, g))
                    for h2 in range(8):
                        blocks.append(("q", 4096 + 256 * h2, 0, h2))
                        blocks.append(("ga", 10240 + 256 * h2, 0, h2))
                    for f2 in range(16):
                        blocks.append(("mp", 12288 + 256 * f2, 0, f2))
                    for f2 in range(16):
                        blocks.append(("ma", 16384 + 256 * f2, 0, f2))
                    w_free = [list(s2a_done), list(s2a_done)]
                    wtoks = {}

                    def load_w2(bi):
                        col0 = blocks[bi][1]
                        wtoks[bi] = dma("pool", sem_w2[bi % 2], w2[bi % 2][:], w_in_v[:, :, col0:col0 + 256],
                                        deps=w_free[bi % 2])
                    bank_free = [list(prev_stage_toks) for _ in range(6)]
                    stg_free = [list(s2a_done) for _ in range(4)]
                    U_free = [list(s2a_done), list(s2a_done)]
                    TA_free = [list(s2a_done), list(s2a_done)]
                    sg_toks = [None] * 8
                    diff_toks = {}
                    diff_free = list(s2a_done)
                    sg_free = list(s2a_done)
                    nb = 0
                    nstg = 0
                    nU = 0
                    s2_out = {}
                    yp_toks = []
                    load_w2(0)
                    for bi, (typ, col0, cidx0, meta) in enumerate(blocks):
                        if bi + 1 < len(blocks):
                            load_w2(bi + 1)
                        W = w2[bi % 2]
                        lastmm = None
                        for sbi in range(2):
                            t0 = sbi * 528 + 16
                            for c in range(2):
                                bk = nb % 4
                                nb += 1
                                bank = ps[bk]
                                for kc in range(32):
                                    deps = [wtoks[bi]] + hTo_ready + bank_free[bk] if kc == 0 else ()
                                    mm = op("pe", lambda e, kc=kc, c=c: e.matmul(
                                        bank[:, :], lhsT=W[:, kc, c * 128:(c + 1) * 128], rhs=hTo[:, kc, t0:t0 + 512],
                                        start=(kc == 0), stop=(kc == 31)), deps=deps, signal=(kc == 31))
                                lastmm = mm
                                cg = cidx0 + c
                                tsl = slice(sbi * 512, (sbi + 1) * 512)
                                if typ == "gp":
                                    e1 = op("act", lambda e: e.activation(out=SG[:, cg, tsl], in_=bank[:, :], func=AF.Silu),
                                            deps=[mm] + sg_free)
                                    sg_toks[cg * 2 + sbi] = e1
                                    bank_free[bk] = [e1]
                                elif typ == "u":
                                    g = meta
                                    for kc in range(32):
                                        deps = bank_free[4] if kc == 0 else ()
                                        mh = op("pe", lambda e, kc=kc, c=c: e.matmul(
                                            ps[4][:, 0:16], lhsT=W[:, kc, c * 128:(c + 1) * 128],
                                            rhs=hTo[:, kc, sbi * 528:sbi * 528 + 16],
                                            start=(kc == 0), stop=(kc == 31)), deps=deps, signal=(kc == 31))
                                    lastmm = mh
                                    ub = nU % 2
                                    nU += 1
                                    Ub = U[ub]
                                    e1 = op("act", lambda e: e.copy(out=Ub[:, 16:528], in_=bank[:, :]),
                                            deps=[mm] + U_free[ub])
                                    e2 = op("act", lambda e: e.copy(out=Ub[:, 0:16], in_=ps[4][:, 0:16]), deps=[mh])
                                    bank_free[bk] = [e1]
                                    bank_free[4] = [e2]
                                    src = Ub
                                    dep = [e1, e2]
                                    ta_used = []
                                    for st in range(g + 1):
                                        sh = 1 << st
                                        lo = (1 << (st + 1)) - 1
                                        dstb = TA[st % 2]
                                        dd = op("dve", lambda e, src=src, dstb=dstb, lo=lo, sh=sh: e.tensor_tensor(
                                            out=dstb[:, lo:528], in0=src[:, lo:528], in1=src[:, lo - sh:528 - sh],
                                            op=ALU.add), deps=dep + TA_free[st % 2])
                                        dep = [dd]
                                        src = dstb
                                    wsum = src
                                    w = WINS[g]
                                    dz = op("dve", lambda e: e.scalar_tensor_tensor(
                                        out=diffT[:, cg, tsl], in0=wsum[:, 16:528], scalar=1.0 / w, in1=Ub[:, 16:528],
                                        op0=ALU.mult, op1=ALU.subtract), deps=dep + diff_free)
                                    last = dz
                                    if hf == 0 and sbi == 0:
                                        f1 = op("dve", lambda e: e.tensor_tensor(
                                            out=tmp16[:], in0=wsum[:, 16:32], in1=rc0[:, g, :], op=ALU.mult),
                                            deps=[dz, ctok_sp])
                                        f2_ = op("dve", lambda e: e.tensor_tensor(
                                            out=diffT[:, cg, 0:16], in0=tmp16[:], in1=Ub[:, 16:32], op=ALU.subtract),
                                            deps=[f1])
                                        last = f2_
                                    TA_free[0] = [last]
                                    TA_free[1] = [last]
                                    U_free[ub] = [last]
                                    diff_toks[(cg, sbi)] = last
                                    if cidx0 == 2 and c == 1:
                                        pm_last = None
                                        for oc in range(4):
                                            for kc4 in range(4):
                                                deps = ([diff_toks[(k, sbi)] for k in range(4)] + bank_free[5] + [gwtok]) if kc4 == 0 else ()
                                                pm = op("pe", lambda e, oc=oc, kc4=kc4: e.matmul(
                                                    ps[5][:, :], lhsT=gwsb[:, g, kc4, oc * 128:(oc + 1) * 128],
                                                    rhs=diffT[:, kc4, tsl], start=(kc4 == 0), stop=(kc4 == 3)),
                                                    deps=deps, signal=(kc4 == 3))
                                            yy = op("dve", lambda e, oc=oc: e.scalar_tensor_tensor(
                                                out=ypool[:, g * 4 + oc, tsl], in0=ps[5][:, :],
                                                scalar=pscale_sb[:, g * 4 + oc:g * 4 + oc + 1], in1=SG[:, oc, tsl],
                                                op0=ALU.mult, op1=ALU.mult),
                                                deps=[pm, sg_toks[oc * 2 + sbi], ctok_sp])
                                            bank_free[5] = [yy]
                                            yp_toks.append(yy)
                                            pm_last = pm
                                        lastmm = pm_last
                                        if sbi == 1:
                                            diff_free = [pm_last]
                                            sg_free = [yy]
                                else:
                                    sl = nstg % 4
                                    nstg += 1
                                    if typ == "q":
                                        e1 = op("act", lambda e: e.activation(out=stg[sl][:], in_=bank[:, :], func=AF.Copy,
                                                                               scale=float(128.0 ** -0.5)),
                                                deps=[mm] + stg_free[sl])
                                        dst = qT_s[:, meta * 2 + c, tsl]
                                    elif typ == "ga":
                                        e1 = op("act", lambda e: e.activation(out=stg[sl][:], in_=bank[:, :], func=AF.Silu),
                                                deps=[mm] + stg_free[sl])
                                        dst = sga_s[:, meta * 2 + c, tsl]
                                    else:
                                        e1 = op("act", lambda e: e.activation(out=stg[sl][:], in_=bank[:, :], func=AF.Sigmoid),
                                                deps=[mm] + stg_free[sl])
                                        dst = (smp_s if typ == "mp" else sma_s)[:, meta * 2 + c, tsl]
                                    bank_free[bk] = [e1]
                                    od = dma("sp", sem_stg[sl], dst, stg[sl][:], deps=[e1])
                                    stg_free[sl] = [od]
                                    s2_out[od[0]] = od
                        w_free[bi % 2] = [lastmm]
                prev_stage_toks = barrier()

                yattn = sbh("yattn%d" % hf, [128, 16, 1024], BF16)
                sB = ExitStack()
                with sB:
                    def sbB(name, shape, dt):
                        return sB.enter_context(nc.sbuf_tensor(name, list(shape), dt))
                    NBLK = 16 if hf == 0 else 32
                    KT = [sbB("KT%d_%d" % (hf, i), [128, NBLK * 256], BF16) for i in range(2)]
                    VV = [sbB("VV%d_%d" % (hf, i), [128, NBLK * 2, 128], BF16) for i in range(2)]
                    QT = [sbB("QT%d_%d" % (hf, i), [128, 1024], BF16) for i in range(2)]
                    GA = [sbB("GA%d_%d" % (hf, i), [128, 1024], BF16) for i in range(2)]
                    PT = [sbB("PT%d_%d" % (hf, i), [128, 512], BF16) for i in range(3)]
                    onehot = sbB("onehot%d" % hf, [128, 4096], BF16)
                    cmask = sbB("cmask%d" % hf, [128, 16, 512], BF16)
                    gm = sbB("gm%d" % hf, [128, 32], F32)
                    m8 = sbB("m8_%d" % hf, [128, 8], F32)
                    t2b = sbB("t2b%d" % hf, [128, 32], F32)
                    selb = [sbB("selb%d_%d" % (hf, i), [128, 4, 32], BF16) for i in range(2)]
                    selT = [sbB("selT%d_%d" % (hf, i), [128, 4, 128], BF16) for i in range(2)]
                    zt0 = op("dve", lambda e: e.memset(selT[0][:], 0.0), deps=prev_stage_toks)
                    zt1 = op("dve", lambda e: e.memset(selT[1][:], 0.0), deps=prev_stage_toks)
                    rden = sbB("rden%d" % hf, [128, 512], F32)
                    onum = sbB("onum%d" % hf, [128, 512], F32)
                    dacc = sbB("dacc%d" % hf, [128, 512], F32)
                    dacc2 = [dacc, sbB("daccb%d" % hf, [128, 512], F32)]
                    PTx = sbB("PTx%d" % hf, [128, 512], BF16)
                    ptsum = sbB("ptsum%d" % hf, [128, 512], BF16)
                    ones32 = sbB("ones32_%d" % hf, [128, 128], F32)
                    o32tok = dma("sp", mksem("Bo32"), ones32[:], ones_in, deps=prev_stage_toks)
                    dacc_free = list(prev_stage_toks) + [o32tok]
                    sem_kv = [mksem("Bkv_%d" % i) for i in range(2)]
                    sem_q = [mksem("Bq_%d" % i) for i in range(2)]
                    sem_cm = mksem("Bcm")
                    cm1 = dma("pool", sem_cm, onehot[:], onehot_in, deps=prev_stage_toks)
                    cm2 = dma("pool", sem_cm, cmask[:], cmask_in)
                    cmtok = cm2
                    kv_free = [list(prev_stage_toks), list(prev_stage_toks)]
                    q_free = [list(prev_stage_toks), list(prev_stage_toks)]
                    kvtoks = {}
                    qtoks = {}

                    def load_head(h):
                        s_ = h % 2
                        dma("sp", sem_kv[s_], KT[s_][:], kT_all[:, h, 0:NBLK * 256], deps=kv_free[s_])
                        kvtoks[h] = dma("sp", sem_kv[s_], VV[s_][:], v_all_v[:, 0:NBLK * 2, h * 128:(h + 1) * 128])
                        dma("sp", sem_q[s_], QT[s_][:], qT_s[:, h, :], deps=q_free[s_])
                        qtoks[h] = dma("sp", sem_q[s_], GA[s_][:], sga_s[:, h, :])

                    S_free = [list(prev_stage_toks) for _ in range(3)]
                    PT_free = [list(prev_stage_toks) for _ in range(3)]
                    OD_free = list(prev_stage_toks)
                    gate_free = list(prev_stage_toks)
                    pbB_free = list(prev_stage_toks)
                    ya_toks = []
                    load_head(0)
                    gidx = 0
                    iters = [(h, k) for h in range(16) for k in range(2)]
                    stk_tok = {}
                    gchain = {}

                    def hdeps(h):
                        return [kvtoks[h], qtoks[h], cmtok, ctok_pool]

                    def gate_p1(j):
                        nonlocal gate_free
                        h, k = iters[j]
                        slot = 2 * hf + k
                        Q_ = QT[h % 2]
                        sbuf_ = selb[j % 2]
                        last = None
                        gmms = []
                        for qt in range(4):
                            gmms.append(op("pe", lambda e, qt=qt: e.matmul(
                                ps[4][:, qt * 32:(qt + 1) * 32], lhsT=Q_[:, k * 512 + qt * 128:k * 512 + (qt + 1) * 128],
                                rhs=kmean[:, h * 32:(h + 1) * 32], start=True, stop=True),
                                deps=hdeps(h) + gate_free + den_free))
                        for qt in range(4):
                            mi = slot * 2 + qt // 2
                            gmm = gmms[qt]
                            g1_ = op("dve", lambda e, qt=qt: e.tensor_tensor(out=gm[:], in0=ps[4][:, qt * 32:(qt + 1) * 32],
                                                                              in1=pastmask[:, mi, :], op=ALU.add),
                                     deps=[gmm, ctok_sp])
                            if qt == 3:
                                gate_free = [g1_]
                            g2_ = op("dve", lambda e: e.max(out=m8[:], in_=gm[:]), deps=[g1_])
                            g3_ = op("dve", lambda e: e.scalar_tensor_tensor(
                                out=t2b[:], in0=gm[:], scalar=m8[:, 2:3], in1=pastvalid[:, mi, :],
                                op0=ALU.is_ge, op1=ALU.mult), deps=[g2_])
                            last = op("dve", lambda e, qt=qt: e.scalar_tensor_tensor(
                                out=sbuf_[:, qt, :], in0=t2b[:], scalar=-NEG, in1=ownbias[:, mi, :],
                                op0=ALU.mult, op1=ALU.add), deps=[g3_] + selb_free[j % 2])
                        gchain[j] = last

                    def gate_p2(j):
                        nonlocal pbB_free
                        sbuf_ = selb[j % 2]
                        tp = None
                        for qt in range(4):
                            tp = op("pe", lambda e, qt=qt: e.transpose(
                                out=pb[0][0:32, qt, :], in_=sbuf_[:, qt, :], identity=ident[:]),
                                deps=[gchain[j]] + pbB_free, signal=(qt == 3))
                        selb_free[j % 2] = [tp]
                        stk = op("dve", lambda e: e.tensor_copy(out=selT[j % 2][0:32, :, :], in_=pb[0][0:32, 0:4, :]),
                                 deps=[tp] + selT_free[j % 2])
                        pbB_free = [stk]
                        stk_tok[j] = stk

                    selb_free = [list(prev_stage_toks), list(prev_stage_toks)]
                    selT_free = [[zt0], [zt1]]
                    den_free = list(prev_stage_toks)
                    OB = [ps[3], ps[5]]
                    OB_free = [list(prev_stage_toks), list(prev_stage_toks)]
                    dacc_free2 = [list(dacc_free), list(dacc_free)]
                    PT4 = PT + [PTx]
                    PT_free = [list(prev_stage_toks) for _ in range(4)]
                    gate_p1(0)
                    gate_p2(0)

                    class It:
                        pass
                    its = []
                    for j, (h, k) in enumerate(iters):
                        it_ = It()
                        it_.j, it_.h, it_.k = j, h, k
                        it_.slot = 2 * hf + k
                        it_.qsl = slice(k * 512, (k + 1) * 512)
                        it_.chunks = [(n, c) for n in range(8 * it_.slot + 8) for c in range(2)]
                        its.append(it_)
                    items = [(it_, idx) for it_ in its for idx in range(len(it_.chunks))]
                    s_tok = {}

                    def emit_S(g):
                        it_, idx = items[g]
                        h, slot = it_.h, it_.slot
                        K_, Q_ = KT[h % 2], QT[h % 2]
                        n, c = it_.chunks[idx]
                        bk = g % 3
                        cc = 2 * n + c
                        diag = n >= 8 * slot
                        op("pe", lambda e: e.matmul(ps[bk][:, :], lhsT=K_[:, cc * 128:(cc + 1) * 128],
                                                    rhs=Q_[:, it_.qsl], start=True, stop=False),
                           deps=hdeps(h) + S_free[bk], signal=False)
                        t_ = op("pe", lambda e: e.matmul(ps[bk][:, :], lhsT=onehot[:, n * 128:(n + 1) * 128],
                                                         rhs=selT[it_.j % 2][:, :, :], start=False, stop=(not diag)),
                                deps=[stk_tok[it_.j]], signal=(not diag))
                        if diag:
                            m = n - 8 * slot
                            t_ = op("pe", lambda e: e.matmul(ps[bk][:, :], lhsT=ident[:, :],
                                                             rhs=cmask[:, m * 2 + c, :], start=False, stop=True))
                        s_tok[g] = t_

                    fin = {}

                    def finalize(it_):
                        nonlocal den_free
                        j, h = it_.j, it_.h
                        G_ = GA[h % 2]
                        dm = op("pe", lambda e: e.matmul(ps[4][:, :], lhsT=ones32[:, :], rhs=dacc2[j % 2][:], start=True, stop=True),
                                deps=[it_.ac] + den_free + gate_free)
                        dacc_free2[j % 2] = [dm]
                        f1 = op("dve", lambda e: e.reciprocal(out=rden[:], in_=ps[4][:, :]), deps=[dm])
                        den_free = [f1]
                        f2 = op("dve", lambda e: e.tensor_tensor(out=onum[:], in0=OB[j % 2][:, :], in1=rden[:], op=ALU.mult),
                                deps=[f1, it_.pv])
                        OB_free[j % 2] = [f2]
                        f3 = op("dve", lambda e: e.tensor_tensor(out=yattn[:, h, it_.qsl], in0=onum[:], in1=G_[:, it_.qsl],
                                                                 op=ALU.mult), deps=[f2])
                        if it_.k == 1:
                            kv_free[h % 2] = [it_.pv]
                            q_free[h % 2] = [it_.pv, f3]

                    emit_S(0)
                    emit_S(1)
                    for g, (it_, idx) in enumerate(items):
                        j, h, k = it_.j, it_.h, it_.k
                        NCH = len(it_.chunks)
                        V_ = VV[h % 2]
                        if g + 2 < len(items):
                            emit_S(g + 2)
                        if idx == 0 and j + 1 < len(its):
                            gate_p1(j + 1)
                        if idx == 3 and j > 0:
                            finalize(its[j - 1])
                        if idx == 4 and k == 0 and h + 1 < 16:
                            load_head(h + 1)
                        if idx == 7 and j + 1 < len(its):
                            gate_p2(j + 1)
                        n, c = it_.chunks[idx]
                        cc = 2 * n + c
                        bk = g % 3
                        pi = g % 4
                        ex = op("act", lambda e: e.activation(out=PT4[pi][:], in_=ps[bk][:, :], func=AF.Exp),
                                deps=[s_tok.pop(g)] + PT_free[pi])
                        S_free[bk] = [ex]
                        pv = op("pe", lambda e: e.matmul(OB[j % 2][:, :], lhsT=V_[:, cc, :], rhs=PT4[pi][:],
                                                         start=(idx == 0), stop=(idx == NCH - 1)),
                                deps=[ex] + (OB_free[j % 2] if idx == 0 else []))
                        if idx % 2 == 0:
                            it_.ex_prev = ex
                            it_.pv_prev = pv
                        else:
                            pp = (g - 1) % 4
                            pa = op("dve", lambda e: e.tensor_tensor(out=ptsum[:], in0=PT4[pp][:], in1=PT4[pi][:], op=ALU.add),
                                    deps=[it_.ex_prev, ex])
                            if idx == 1:
                                ac = op("dve", lambda e: e.tensor_copy(out=dacc2[j % 2][:], in_=ptsum[:]),
                                        deps=[pa] + dacc_free2[j % 2])
                            else:
                                ac = op("dve", lambda e: e.tensor_tensor(out=dacc2[j % 2][:], in0=dacc2[j % 2][:], in1=ptsum[:],
                                                                         op=ALU.add), deps=[pa, it_.ac])
                            PT_free[pp] = [it_.pv_prev, pa]
                            PT_free[pi] = [pv, pa]
                            it_.ac = ac
                        it_.pv = pv
                        if idx == NCH - 1:
                            selT_free[j % 2] = [pv]
                    finalize(its[-1])
                prev_stage_toks = barrier()

                mergedT = sbh("mergedT%d" % hf, [128, 32, 1024], BF16)
                sC = ExitStack()
                with sC:
                    def sbC(name, shape, dt):
                        return sC.enter_context(nc.sbuf_tensor(name, list(shape), dt))
                    wP = [sbC("wP%d_%d" % (hf, i), [128, 16, 256], BF16) for i in range(2)]
                    wA = [sbC("wA%d_%d" % (hf, i), [128, 16, 256], BF16) for i in range(2)]
                    gP = [sbC("gP%d_%d" % (hf, i), [128, 2, 1024], BF16) for i in range(2)]
                    gA = [sbC("gA%d_%d" % (hf, i), [128, 2, 1024], BF16) for i in range(2)]
                    t1s = [sbC("t1s%d_%d" % (hf, i), [128, 512], F32) for i in range(2)]
                    sem_wc = [mksem("Cw_%d" % i) for i in range(2)]
                    sem_gc = [mksem("Cg_%d" % i) for i in range(2)]
                    wc_free = [list(prev_stage_toks), list(prev_stage_toks)]
                    gc_free = [list(prev_stage_toks), list(prev_stage_toks)]
                    wct = {}
                    gct = {}

                    def load_c(bi):
                        s_ = bi % 2
                        dma("pool", sem_wc[s_], wP[s_][:], wpo_v[:, :, bi * 256:(bi + 1) * 256], deps=wc_free[s_])
                        wct[bi] = dma("pool", sem_wc[s_], wA[s_][:], wao_v[:, :, bi * 256:(bi + 1) * 256])
                        dma("sp", sem_gc[s_], gP[s_][:], smp_s[:, bi * 2:bi * 2 + 2, :], deps=gc_free[s_])
                        gct[bi] = dma("sp", sem_gc[s_], gA[s_][:], sma_s[:, bi * 2:bi * 2 + 2, :])
                    bank_free = [list(prev_stage_toks) for _ in range(4)]
                    t1_free = [list(prev_stage_toks), list(prev_stage_toks)]
                    load_c(0)
                    ncc = 0
                    mg_toks = []
                    for bi in range(16):
                        if bi + 1 < 16:
                            load_c(bi + 1)
                        s_ = bi % 2
                        lastmm = None
                        lastd = None
                        for sbi in range(2):
                            tsl = slice(sbi * 512, (sbi + 1) * 512)
                            for c in range(2):
                                f = bi * 2 + c
                                b1 = (ncc % 2) * 2
                                b2 = b1 + 1
                                ti = ncc % 2
                                ncc += 1
                                for kc in range(16):
                                    deps = [wct[bi]] + bank_free[b1] if kc == 0 else ()
                                    m1 = op("pe", lambda e, kc=kc: e.matmul(
                                        ps[b1][:, :], lhsT=wP[s_][:, kc, c * 128:(c + 1) * 128], rhs=ypool[:, kc, tsl],
                                        start=(kc == 0), stop=(kc == 15)), deps=deps, signal=(kc == 15))
                                for kc in range(16):
                                    deps = bank_free[b2] if kc == 0 else ()
                                    m2 = op("pe", lambda e, kc=kc: e.matmul(
                                        ps[b2][:, :], lhsT=wA[s_][:, kc, c * 128:(c + 1) * 128], rhs=yattn[:, kc, tsl],
                                        start=(kc == 0), stop=(kc == 15)), deps=deps, signal=(kc == 15))
                                lastmm = m2
                                d1 = op("dve", lambda e: e.tensor_tensor(out=t1s[ti][:], in0=ps[b1][:, :], in1=gP[s_][:, c, tsl],
                                                                         op=ALU.mult), deps=[m1, gct[bi]] + t1_free[ti])
                                d2 = op("dve", lambda e: e.tensor_tensor(out=mergedT[:, f, tsl], in0=ps[b2][:, :],
                                                                         in1=gA[s_][:, c, tsl], op=ALU.mult), deps=[m2])
                                d3 = op("dve", lambda e: e.tensor_tensor(out=mergedT[:, f, tsl], in0=mergedT[:, f, tsl],
                                                                         in1=t1s[ti][:], op=ALU.add), deps=[d2, d1])
                                bank_free[b1] = [d1]
                                bank_free[b2] = [d2]
                                t1_free[ti] = [d3]
                                lastd = d3
                        wc_free[s_] = [lastmm]
                        gc_free[s_] = [lastd]
                        mg_toks.append(lastd)
                prev_stage_toks = barrier()

                sD = ExitStack()
                with sD:
                    def sbD(name, shape, dt):
                        return sD.enter_context(nc.sbuf_tensor(name, list(shape), dt))
                    wD = [sbD("wD%d_%d" % (hf, i), [128, 32, 256], BF16) for i in range(2)]
                    ost = [sbD("ost%d_%d" % (hf, i), [128, 256], F32) for i in range(4)]
                    sem_wd = [mksem("Dw_%d" % i) for i in range(2)]
                    sem_os = [mksem("Do_%d" % i) for i in range(4)]
                    wd_free = [list(prev_stage_toks), list(prev_stage_toks)]
                    wdt = {}

                    def load_d(bi):
                        wdt[bi] = dma("pool", sem_wd[bi % 2], wD[bi % 2][:], wo_v[:, :, bi * 256:(bi + 1) * 256],
                                      deps=wd_free[bi % 2])
                    bank_free = [list(prev_stage_toks) for _ in range(4)]
                    ost_free = [list(prev_stage_toks) for _ in range(4)]
                    load_d(0)
                    nd = 0
                    for bi in range(16):
                        if bi + 1 < 16:
                            load_d(bi + 1)
                        W = wD[bi % 2]
                        for tile in range(8):
                            bk = nd % 4
                            sl = nd % 4
                            nd += 1
                            for kc in range(32):
                                deps = [wdt[bi]] + bank_free[bk] if kc == 0 else ()
                                mm = op("pe", lambda e, kc=kc: e.matmul(
                                    ps[bk][:, 0:256], lhsT=mergedT[:, kc, tile * 128:(tile + 1) * 128], rhs=W[:, kc, :],
                                    start=(kc == 0), stop=(kc == 31)), deps=deps, signal=(kc == 31))
                            if nd % 2 == 0:
                                d1 = op("dve", lambda e: e.tensor_copy(out=ost[sl][:], in_=ps[bk][:, 0:256]),
                                        deps=[mm] + ost_free[sl])
                            else:
                                d1 = op("act", lambda e: e.copy(out=ost[sl][:], in_=ps[bk][:, 0:256]),
                                        deps=[mm] + ost_free[sl])
                            bank_free[bk] = [d1]
                            od = dma("sp", sem_os[sl], O_s[tile, :, bi * 256:(bi + 1) * 256], ost[sl][:], deps=[d1])
                            ost_free[sl] = [od]
                        wd_free[bi % 2] = [mm]
                prev_stage_toks = barrier()
            hs2 = ExitStack()
            with hs2:
                def sbe(name, shape, dt):
                    return hs2.enter_context(nc.sbuf_tensor(name, list(shape), dt))
                x1T = sbe("x1T%d" % hf, [128, 32, 1024], BF16)
                sE1 = ExitStack()
                with sE1:
                    def sbE(name, shape, dt):
                        return sE1.enter_context(nc.sbuf_tensor(name, list(shape), dt))
                    g2 = sbE("g2_%d" % hf, [128, D], F32)
                    Ob = [sbE("Ob%d_%d" % (hf, i), [128, D], F32) for i in range(3)]
                    Xb = [sbE("Xb%d_%d" % (hf, i), [128, D], F32) for i in range(3)]
                    x1b = [sbE("x1b%d_%d" % (hf, i), [128, D], BF16) for i in range(2)]
                    rs = sbE("rs%d" % hf, [128, 8], F32)
                    rs2 = sbE("rs2_%d" % hf, [128, 8], F32)
                    sem_o = [mksem("Eo_%d" % i) for i in range(3)]
                    sem_xx = [mksem("Ex_%d" % i) for i in range(3)]
                    sem_x1 = [mksem("Ex1_%d" % i) for i in range(3)]
                    sem_g2 = mksem("Eg2_")
                    g2tok = dma("sp", sem_g2, g2[:], g2rep, deps=prev_stage_toks)
                    o_free = [list(prev_stage_toks) for _ in range(3)]
                    x_free = [list(prev_stage_toks) for _ in range(3)]
                    x1b_free = [list(prev_stage_toks), list(prev_stage_toks)]
                    pb_free = [list(prev_stage_toks), list(prev_stage_toks)]
                    otok = {}
                    xtok = {}

                    def load_e1(tile):
                        b = tile % 3
                        sbi, tq = tile // 4, tile % 4
                        r0 = sbi * 528 + 16 + tq * 128
                        otok[tile] = dma("sp", sem_o[b], Ob[b][:], O_s[tile, :, :], deps=o_free[b])
                        xtok[tile] = dma("sp", sem_xx[b], Xb[b][:], xo[hf, r0:r0 + 128, :], deps=x_free[b])
                    junk = sbE("junkE%d" % hf, [128, D], BF16)
                    x1T_toks = []
                    a3s = {}
                    d3s = {}

                    def e1_p1(tile):
                        b = tile % 3
                        a1 = op("act", lambda e: e.activation(out=junk[:], in_=Ob[b][:], func=AF.Square,
                                                               accum_out=rs[:, tile:tile + 1]),
                                deps=[otok[tile]] + list(prev_stage_toks))
                        a2 = op("act", lambda e: e.activation(out=rs[:, tile:tile + 1], in_=rs[:, tile:tile + 1],
                                                               func=AF.Sqrt, scale=1.0 / D, bias=eps_t[:, 0:1]), deps=[a1])
                        d1 = op("dve", lambda e: e.reciprocal(out=rs2[:, tile:tile + 1], in_=rs[:, tile:tile + 1]), deps=[a2])
                        d2 = op("dve", lambda e: e.scalar_tensor_tensor(out=Ob[b][:], in0=Ob[b][:], scalar=rs2[:, tile:tile + 1],
                                                                        in1=g2[:], op0=ALU.mult, op1=ALU.mult),
                                deps=[d1, g2tok, a1])
                        d3 = op("dve", lambda e: e.tensor_tensor(out=Xb[b][:], in0=Xb[b][:], in1=Ob[b][:], op=ALU.add),
                                deps=[d2, xtok[tile]])
                        o_free[b] = [d3]
                        od = dma("sp", sem_x1[b], x1_s[tile, :, :], Xb[b][:], deps=[d3])
                        d3s[tile] = (d3, od)

                    def e1_p1b(tile):
                        b = tile % 3
                        b2 = tile % 2
                        d3, od = d3s.pop(tile)
                        a3 = op("act", lambda e: e.copy(out=x1b[b2][:], in_=Xb[b][:]), deps=[d3] + x1b_free[b2])
                        x_free[b] = [od, a3]
                        a3s[tile] = a3

                    def e1_p2(tile):
                        b = tile % 2
                        a3 = a3s.pop(tile)
                        tp = None
                        for g8 in range(8):
                            bank = pb[g8 % 2]
                            for j in range(4):
                                kc = g8 * 4 + j
                                tp = op("pe", lambda e, kc=kc, j=j: e.transpose(
                                    out=bank[:, j, :], in_=x1b[b][:, kc * 128:(kc + 1) * 128], identity=ident[:]),
                                    deps=[a3] + pb_free[g8 % 2], signal=(j == 3))
                            dst = x1T[:, g8 * 4:(g8 + 1) * 4, tile * 128:(tile + 1) * 128]
                            ev = op("act", lambda e: e.copy(out=dst, in_=bank[:, 0:4, :]), deps=[tp] + list(prev_stage_toks))
                            pb_free[g8 % 2] = [ev]
                            x1T_toks.append(ev)
                        x1b_free[b] = [tp]

                    load_e1(0)
                    load_e1(1)
                    load_e1(2)
                    e1_p1(0)
                    e1_p1b(0)
                    for tile in range(8):
                        if tile + 1 < 8:
                            e1_p1(tile + 1)
                        e1_p2(tile)
                        if tile + 1 < 8:
                            e1_p1b(tile + 1)
                        if tile + 3 < 8:
                            load_e1(tile + 3)
                prev_stage_toks = barrier()

                sE2 = ExitStack()
                with sE2:
                    def sbF(name, shape, dt):
                        return sE2.enter_context(nc.sbuf_tensor(name, list(shape), dt))
                    wG = [sbF("wG%d_%d" % (hf, i), [128, 32, 512], BF16) for i in range(2)]
                    wPp = [sbF("wPp%d_%d" % (hf, i), [128, 2, 512], BF16) for i in range(2)]
                    pTs = sbF("pTs%d" % hf, [128, 2, 1024], BF16)
                    x1t = [sbF("x1t%d_%d" % (hf, i), [128, 512], F32) for i in range(3)]
                    sgt = [sbF("sgt%d_%d" % (hf, i), [128, 512], F32) for i in range(2)]
                    sem_wg = [mksem("Fw_%d" % i) for i in range(2)]
                    sem_x1t = [mksem("Fx_%d" % i) for i in range(3)]
                    sem_out = [mksem("Fo_%d" % i) for i in range(3)]
                    sem_pt = mksem("Fp")
                    pttok = dma("pool", sem_pt, pTs[:], pT_v[:, :, hf * 1024:(hf + 1) * 1024], deps=prev_stage_toks)
                    wg_free = [list(prev_stage_toks), list(prev_stage_toks)]
                    wgt = {}

                    def load_g(bi):
                        s_ = bi % 2
                        dma("pool", sem_wg[s_], wG[s_][:], wg_v[:, :, bi * 512:(bi + 1) * 512], deps=wg_free[s_])
                        wgt[bi] = dma("pool", sem_wg[s_], wPp[s_][:], wp_v[:, :, bi * 512:(bi + 1) * 512])
                    x1t_free = [list(prev_stage_toks) for _ in range(3)]
                    x1tt = {}
                    items = [(bi, tile) for bi in range(8) for tile in range(8)]

                    def load_x1t(ii):
                        bi, tile = items[ii]
                        s_ = ii % 3
                        x1tt[ii] = dma("sp", sem_x1t[s_], x1t[s_][:], x1_s[tile, :, bi * 512:(bi + 1) * 512],
                                       deps=x1t_free[s_])
                    bank_free = [list(prev_stage_toks) for _ in range(4)]
                    sgt_free = [list(prev_stage_toks), list(prev_stage_toks)]
                    out_toks = {}
                    load_g(0)
                    load_x1t(0)
                    for ii, (bi, tile) in enumerate(items):
                        if tile == 0 and bi + 1 < 8:
                            load_g(bi + 1)
                        s_ = bi % 2
                        bG = (ii % 2) * 2
                        bP = bG + 1
                        for kc in range(32):
                            deps = [wgt[bi]] + bank_free[bG] if kc == 0 else ()
                            mG = op("pe", lambda e, kc=kc: e.matmul(
                                ps[bG][:, :], lhsT=x1T[:, kc, tile * 128:(tile + 1) * 128], rhs=wG[s_][:, kc, :],
                                start=(kc == 0), stop=(kc == 31)), deps=deps, signal=(kc == 31))
                        for kc in range(2):
                            deps = [pttok] + bank_free[bP] if kc == 0 else ()
                            mP = op("pe", lambda e, kc=kc: e.matmul(
                                ps[bP][:, :], lhsT=pTs[:, kc, tile * 128:(tile + 1) * 128], rhs=wPp[s_][:, kc, :],
                                start=(kc == 0), stop=(kc == 1)), deps=deps, signal=(kc == 1))
                        si = ii % 2
                        xi = ii % 3
                        a1 = op("act", lambda e: e.activation(out=sgt[si][:], in_=ps[bG][:, :], func=AF.Sigmoid),
                                deps=[mG] + sgt_free[si])
                        d1 = op("dve", lambda e: e.tensor_tensor(out=sgt[si][:], in0=sgt[si][:], in1=ps[bP][:, :], op=ALU.mult),
                                deps=[a1, mP])
                        d2 = op("dve", lambda e: e.tensor_tensor(out=x1t[xi][:], in0=x1t[xi][:], in1=sgt[si][:], op=ALU.add),
                                deps=[d1, x1tt[ii]])
                        bank_free[bG] = [a1]
                        bank_free[bP] = [d1]
                        sgt_free[si] = [d2]
                        od = dma("sp", sem_out[xi], out_own[hf * 8 + tile, :, bi * 512:(bi + 1) * 512], x1t[xi][:], deps=[d2])
                        x1t_free[xi] = [od]
                        out_toks[od[0]] = od
                        if ii + 1 < len(items):
                            load_x1t(ii + 1)
                        if tile == 7:
                            wg_free[s_] = [mG, mP]
                prev_stage_toks = barrier()
        sc.wait("sp", barrier())
    return nc


_NC_CACHE = {}


def kernel(x, p, norm_pre, w_in, pool_group_w, pool_scale, w_pool_out, w_attn_out,
           w_out, norm_post, w_ple_proj, w_ple_gate):
    f32 = np.float32
    x = np.ascontiguousarray(np.asarray(x, f32))
    p = np.asarray(p, f32)
    w_in0 = np.ascontiguousarray(np.asarray(w_in, f32)[0])
    gw0 = np.ascontiguousarray(np.asarray(pool_group_w, f32)[0])
    wpo0 = np.ascontiguousarray(np.asarray(w_pool_out, f32)[0])
    wao0 = np.ascontiguousarray(np.asarray(w_attn_out, f32)[0])
    wo0 = np.ascontiguousarray(np.asarray(w_out, f32)[0])
    wg0 = np.ascontiguousarray(np.asarray(w_ple_gate, f32)[0])
    wp0 = np.ascontiguousarray(np.asarray(w_ple_proj, f32)[0])
    g1rep = np.ascontiguousarray(np.broadcast_to(np.asarray(norm_pre, f32)[0][None, :], (128, D)))
    g2rep = np.ascontiguousarray(np.broadcast_to(np.asarray(norm_post, f32)[0][None, :], (128, D)))
    pscale = np.ascontiguousarray(np.asarray(pool_scale, f32)[0].reshape(16, 128).T)
    ident = np.eye(128, dtype=f32)
    ones = np.ones((128, 128), f32)
    onehot = np.zeros((128, 32, 128), f32)
    for n in range(32):
        onehot[n, n, :] = 1.0
    onehot = onehot.reshape(128, 4096)

    if "nc" not in _NC_CACHE:
        _NC_CACHE["nc"] = build_program()
    nc = _NC_CACHE["nc"]

    in_maps = []
    own_tokens = []
    for c in range(8):
        b, r = c // 4, c % 4
        xo = np.zeros((2, 1152, D), f32)
        toks = []
        for i in range(4):
            hf, k = i // 2, i % 2
            s = 4 * i + r
            t0 = s * 512
            base = k * 528
            if t0 > 0:
                xo[hf, base:base + 16] = x[b, t0 - 16:t0]
            xo[hf, base + 16:base + 528] = x[b, t0:t0 + 512]
            toks.append(np.arange(t0, t0 + 512))
        toks = np.concatenate(toks)
        own_tokens.append(toks)
        pT = np.ascontiguousarray(p[0, b][toks, :].T)
        cmask = np.zeros((128, 16, 512), f32)
        pidx = np.arange(128)[:, None]
        qpos = (np.arange(512) % 256)[None, :]
        half = (np.arange(512) // 256)[None, :]
        for m in range(8):
            for cc in range(2):
                keyb = cc * 128 + pidx
                own = (2 * r + half) == m
                cmask[:, m * 2 + cc, :] = np.where(own & (keyb > qpos), NEG, 0.0)
        pastmask = np.zeros((128, 8, 32), f32)
        pastvalid = np.zeros((128, 8, 32), f32)
        ownbias = np.zeros((128, 8, 32), f32)
        nidx = np.arange(32)
        for slot in range(4):
            for qb in range(2):
                j = 2 * (4 * slot + r) + qb
                pastvalid[:, slot * 2 + qb, :] = (nidx < j).astype(f32)[None, :]
                pastmask[:, slot * 2 + qb, :] = np.where(nidx < j, 0.0, -1e30)[None, :]
                ownbias[:, slot * 2 + qb, :] = np.where(nidx == j, 0.0, NEG)[None, :]
        rc0 = np.zeros((128, 4, 16), f32)
        for g, w in enumerate(WINS):
            if r == 0:
                rc0[:, g, :] = (1.0 / np.minimum(np.arange(16) + 1, w)).astype(f32)[None, :]
            else:
                rc0[:, g, :] = 1.0 / w
        in_maps.append({
            "xs": x[b], "xo": xo, "pT": pT, "w_in": w_in0, "gw": gw0, "wpo": wpo0, "wao": wao0,
            "wo": wo0, "wg": wg0, "wp": wp0, "g1rep": g1rep, "g2rep": g2rep, "pscale": pscale,
            "ident": ident, "ones": ones, "onehot": onehot, "cmask": cmask, "pastmask": pastmask,
            "pastvalid": pastvalid, "ownbias": ownbias, "rc0": rc0,
        })
    res = run_bass_kernel_spmd(nc, in_maps, core_ids=list(range(8)))
    out = np.empty((2, S, D), f32)
    for c in range(8):
        b = c // 4
        o = np.asarray(res.results[c]["out_own"]).reshape(2048, D)
        out[b, own_tokens[c], :] = o
    if DEBUG_OUTS:
        kernel.debug = [{k: np.asarray(res.results[c][k]) for k in DEBUG_OUTS} for c in range(8)]
    return out
```

```python
import numpy as np
import concourse.bass as bass
import concourse.mybir as mybir
from concourse.bass_utils import run_bass_kernel_spmd

F32 = mybir.dt.float32
BF16 = mybir.dt.bfloat16
AF = mybir.ActivationFunctionType
ALU = mybir.AluOpType
AX = mybir.AxisListType

D = 4096
S = 8192
NEG = -30000.0
EPS = 1e-6
WINS = (2, 4, 8, 16)
DEBUG_OUTS = ()


class Sched:
    def __init__(self, nc, prog):
        self.nc = nc
        self.eng = {"pe": nc.tensor, "act": nc.scalar, "dve": nc.vector, "pool": nc.gpsimd, "sp": nc.sync}
        self.prog = prog
        self.cnt = {k: 0 for k in prog}
        self.waited = {k: {} for k in self.eng}
        self.dcnt = {}

    def wait(self, e, toks):
        w = self.waited[e]
        for t in toks:
            if t is None:
                continue
            name, sem, val = t
            if w.get(name, 0) < val:
                self.eng[e].wait_ge(sem, val)
                w[name] = val

    def op(self, e, fn, deps=(), signal=True):
        self.wait(e, deps)
        ins = fn(self.eng[e])
        if signal:
            self.cnt[e] += 1
            ins.then_inc(self.prog[e][1], 1)
            return (self.prog[e][0], self.prog[e][1], self.cnt[e])
        return None

    def dma(self, q, sem, out, in_, deps=()):
        self.wait(q, deps)
        ins = self.eng[q].dma_start(out=out, in_=in_)
        ins.then_inc(sem[1], 16)
        self.dcnt[sem[0]] = self.dcnt.get(sem[0], 0) + 16
        return (sem[0], sem[1], self.dcnt[sem[0]])


def build_program():
    nc = bass.Bass("TRN2", target_bir_lowering=False)

    def din(name, shape, dt=F32):
        return nc.dram_tensor(name, list(shape), dt, kind="ExternalInput").ap()

    def scratch(name, shape, dt):
        kind = "ExternalOutput" if name in DEBUG_OUTS else "Internal"
        return nc.dram_tensor(name, list(shape), dt, kind=kind).ap()

    xs = din("xs", [S, D])
    xo = din("xo", [2, 1152, D])
    pT = din("pT", [256, 2048])
    w_in = din("w_in", [D, 20480])
    gw = din("gw", [4, 512, 512])
    wpo = din("wpo", [2048, D])
    wao = din("wao", [2048, D])
    wo = din("wo", [D, D])
    wg = din("wg", [D, D])
    wp = din("wp", [256, D])
    g1rep = din("g1rep", [128, D])
    g2rep = din("g2rep", [128, D])
    pscale = din("pscale", [128, 16])
    ident_in = din("ident", [128, 128])
    ones_in = din("ones", [128, 128])
    onehot_in = din("onehot", [128, 4096])
    cmask_in = din("cmask", [128, 16, 512])
    pastmask_in = din("pastmask", [128, 8, 32])
    pastvalid_in = din("pastvalid", [128, 8, 32])
    ownbias_in = din("ownbias", [128, 8, 32])
    rc0_in = din("rc0", [128, 4, 16])
    out_own = nc.dram_tensor("out_own", [16, 128, D], F32, kind="ExternalOutput").ap()

    hT_all = scratch("hT_all", [128, 32, S], BF16)
    kT_all = scratch("kT_all", [128, 16, S], BF16)
    v_all = scratch("v_all", [64, 128, 2048], BF16)
    qT_s = scratch("qT_s", [128, 16, 1024], BF16)
    sga_s = scratch("sga_s", [128, 16, 1024], BF16)
    smp_s = scratch("smp_s", [128, 32, 1024], BF16)
    sma_s = scratch("sma_s", [128, 32, 1024], BF16)
    O_s = scratch("O_s", [8, 128, D], F32)
    x1_s = scratch("x1_s", [8, 128, D], F32)

    w_in_v = w_in.rearrange("(kc p) n -> p kc n", p=128)
    gw_v = gw.rearrange("g (kc p) n -> p g kc n", p=128)
    wpo_v = wpo.rearrange("(kc p) n -> p kc n", p=128)
    wao_v = wao.rearrange("(kc p) n -> p kc n", p=128)
    wo_v = wo.rearrange("(kc p) n -> p kc n", p=128)
    wg_v = wg.rearrange("(kc p) n -> p kc n", p=128)
    wp_v = wp.rearrange("(kc p) n -> p kc n", p=128)
    pT_v = pT.rearrange("(kc p) n -> p kc n", p=128)
    v_all_v = v_all.rearrange("c p f -> p c f")

    from contextlib import ExitStack
    es = ExitStack()
    with es:
        def sb(name, shape, dt):
            return es.enter_context(nc.sbuf_tensor(name, list(shape), dt))

        semreg = {}

        def mksem(name):
            if name not in semreg:
                semreg[name] = (name, es.enter_context(nc.semaphore(name)))
            return semreg[name]

        prog = {k: mksem("prog_" + k) for k in ("pe", "act", "dve", "pool")}
        ps = [es.enter_context(nc.psum_tensor("ps%d" % i, [128, 512], F32)) for i in range(6)]
        pb = [es.enter_context(nc.psum_tensor("pb%d" % i, [128, 8, 128], BF16)) for i in range(2)]
        ident = sb("ident_sb", [128, 128], BF16)
        ones = sb("ones_sb", [128, 128], BF16)
        ksum = sb("ksum", [128, 512], F32)
        kmean = sb("kmean", [128, 512], BF16)
        stat = sb("stat", [128, 64], F32)
        rinv = sb("rinv", [128, 64], F32)
        pscale_sb = sb("pscale_sb", [128, 16], F32)
        pastmask = sb("pastmask_sb", [128, 8, 32], F32)
        pastvalid = sb("pastvalid_sb", [128, 8, 32], F32)
        ownbias = sb("ownbias_sb", [128, 8, 32], F32)
        rc0 = sb("rc0_sb", [128, 4, 16], F32)

        block = es.enter_context(nc.Block())
        sc = Sched(nc, prog)
        op, dma = sc.op, sc.dma

        def barrier():
            toks = [(prog[k][0], prog[k][1], sc.cnt[k]) for k in prog if sc.cnt[k] > 0]
            toks += [(n, semreg[n][1], v) for n, v in sc.dcnt.items()]
            return toks

        sem_c = mksem("const_sp")
        sem_cp = mksem("const_pool")
        ctoks = []
        for dst, src in ((pscale_sb, pscale), (pastmask, pastmask_in), (pastvalid, pastvalid_in),
                         (ownbias, ownbias_in), (rc0, rc0_in)):
            ctoks.append(dma("sp", sem_c, dst[:], src))
        ctok_sp = ctoks[-1]
        t1 = dma("pool", sem_cp, ident[:], ident_in)
        t2 = dma("pool", sem_cp, ones[:], ones_in)
        ctok_pool = t2

        eps_t = sb("eps_t", [128, 1], F32)
        eps_tok = op("dve", lambda e: e.memset(eps_t[:], EPS))

        class NT:
            def __init__(self, stack, tag, grep_dram, init):
                self.g1 = stack.enter_context(nc.sbuf_tensor("g1" + tag, [128, D], F32))
                self.xt = [stack.enter_context(nc.sbuf_tensor("xt%s%d" % (tag, i), [128, D], F32)) for i in range(2)]
                self.xsb = [stack.enter_context(nc.sbuf_tensor("xsb%s%d" % (tag, i), [128, D], BF16)) for i in range(2)]
                self.sem_x = [mksem("ntx%d" % i) for i in range(2)]
                self.sem_g = mksem("ntg")
                self.g1tok = dma("sp", self.sem_g, self.g1[:], grep_dram, deps=init)
                self.xt_free = [list(init), list(init)]
                self.xsb_free = [list(init), list(init)]
                self.pb_free = [list(init) for _ in range(4)]
                self.d2 = {}

            def p1(self, t, src_ap):
                b = t % 2
                xt_ap, xsb_ap = self.xt[b][:], self.xsb[b][:]
                tok_x = dma("sp", self.sem_x[b], xt_ap, src_ap, deps=self.xt_free[b])
                a1 = op("dve", lambda e: e.scalar_tensor_tensor(out=xsb_ap, in0=xt_ap, scalar=1.0, in1=xt_ap,
                                                                op0=ALU.mult, op1=ALU.mult,
                                                                accum_out=stat[:, t:t + 1]),
                        deps=[tok_x, eps_tok] + self.xsb_free[b])
                a2 = op("act", lambda e: e.activation(out=stat[:, t:t + 1], in_=stat[:, t:t + 1],
                                                       func=AF.Sqrt, scale=1.0 / D, bias=eps_t[:, 0:1]), deps=[a1])
                d1 = op("dve", lambda e: e.reciprocal(out=rinv[:, t:t + 1], in_=stat[:, t:t + 1]), deps=[a2])
                d2 = op("dve", lambda e: e.scalar_tensor_tensor(out=xsb_ap, in0=xt_ap, scalar=rinv[:, t:t + 1],
                                                                in1=self.g1[:], op0=ALU.mult, op1=ALU.mult),
                        deps=[d1, a1, self.g1tok])
                self.xt_free[b] = [d2]
                self.d2[t] = d2

            def p2(self, t, evac):
                b = t % 2
                xsb_ap = self.xsb[b][:]
                d2 = self.d2.pop(t)
                tp = None
                for g8 in range(8):
                    sl = g8 % 2
                    bank = pb[sl]
                    hh = 0
                    for j in range(4):
                        kc = g8 * 4 + j
                        tp = op("pe", lambda e, kc=kc, j=j, bank=bank, hh=hh: e.transpose(
                            out=bank[:, hh + j, :], in_=xsb_ap[:, kc * 128:(kc + 1) * 128], identity=ident[:]),
                            deps=[d2, ctok_pool] + self.pb_free[sl], signal=(j == 3))
                    self.pb_free[sl] = evac(g8, bank[:, hh:hh + 4, :], tp)
                self.xsb_free[b] = [tp]

        s1 = ExitStack()
        with s1:
            def sb1(name, shape, dt):
                return s1.enter_context(nc.sbuf_tensor(name, list(shape), dt))
            wsb = [sb1("wsb%d" % i, [128, 32, 512], BF16) for i in range(2)]
            hsb = [sb1("hsb%d" % i, [128, 32, 512], BF16) for i in range(2)]
            kst = [sb1("kst%d" % i, [128, 512], BF16) for i in range(4)]
            sem_w = [mksem("s1w%d" % i) for i in range(2)]
            sem_hh = [mksem("s1h%d" % i) for i in range(2)]
            sem_ho = [mksem("s1o%d" % i) for i in range(2)]
            sem_k = [mksem("s1k%d" % i) for i in range(4)]
            w_free = [[], []]
            h_free = [[], []]
            kst_free = [[], [], [], []]
            bank_free = [[], [], [], []]
            KV0 = 6144
            wtoks = {}
            htoks = {}
            nk = [0]

            def load_w(wb):
                wtoks[wb] = dma("pool", sem_w[wb % 2], wsb[wb % 2][:],
                                w_in_v[:, :, KV0 + wb * 512:KV0 + (wb + 1) * 512], deps=w_free[wb % 2])

            def mm_group(wb, tt, H, hdeps, cs=(0, 1, 2, 3)):
                W = wsb[wb % 2]
                mm = None
                for c in cs:
                    bi = nk[0] % 4
                    bank = ps[bi]
                    for kc in range(32):
                        deps = [wtoks[wb]] + list(hdeps) + bank_free[bi] if kc == 0 else ()
                        if wb < 4:
                            mm = op("pe", lambda e, kc=kc, c=c: e.matmul(
                                bank[:, :], lhsT=W[:, kc, c * 128:(c + 1) * 128], rhs=H[:, kc, :],
                                start=(kc == 0), stop=(kc == 31)), deps=deps, signal=(kc == 31))
                        else:
                            mm = op("pe", lambda e, kc=kc, c=c: e.matmul(
                                bank[:, :], lhsT=H[:, kc, c * 128:(c + 1) * 128], rhs=W[:, kc, :],
                                start=(kc == 0), stop=(kc == 31)), deps=deps, signal=(kc == 31))
                    sl = nk[0] % 4
                    nk[0] += 1
                    if wb < 4:
                        h = wb * 4 + c
                        e1 = op("act", lambda e: e.activation(
                            out=kst[sl][:, 0:256], in_=bank[:, 0:256], func=AF.Copy,
                            accum_out=ksum[:, h * 32 + 2 * tt:h * 32 + 2 * tt + 1]),
                            deps=[mm] + kst_free[sl])
                        e2 = op("act", lambda e: e.activation(
                            out=kst[sl][:, 256:512], in_=bank[:, 256:512], func=AF.Copy,
                            accum_out=ksum[:, h * 32 + 2 * tt + 1:h * 32 + 2 * tt + 2]), deps=[mm])
                        dst = kT_all[:, h, tt * 512:(tt + 1) * 512]
                    else:
                        e2 = op("act", lambda e: e.copy(out=kst[sl][:], in_=bank[:, :]),
                                deps=[mm] + kst_free[sl])
                        dst = v_all[tt * 4 + c, :, (wb - 4) * 512:(wb - 3) * 512]
                    bank_free[bi] = [e2]
                    od = dma("act", sem_k[sl], dst, kst[sl][:], deps=[e2])
                    kst_free[sl] = [od]
                return mm

            load_w(0)
            load_w(1)
            s0 = ExitStack()
            with s0:
                nt = NT(s0, "a", g1rep, [])
                nt.p1(0, xs[0:128, :])
                pend = None
                for T in range(16):
                    hb = T % 2
                    evs = []
                    for tq in range(4):
                        t = 4 * T + tq
                        if t + 1 < 64:
                            nt.p1(t + 1, xs[(t + 1) * 128:(t + 2) * 128, :])

                        def evac(g8, bank, tp, hb=hb, tq=tq):
                            dst = hsb[hb][:, g8 * 4:(g8 + 1) * 4, tq * 128:(tq + 1) * 128]
                            ev = op("act", lambda e: e.copy(out=dst, in_=bank), deps=[tp] + h_free[hb])
                            evs.append(ev)
                            return [ev]
                        nt.p2(t, evac)
                        if pend is not None:
                            pT_, phb, pev, pod = pend
                            lastmm = mm_group(0, pT_, hsb[phb], pev, cs=(tq,))
                            if tq == 3:
                                h_free[phb] = [lastmm, pod]
                    od = dma("act", sem_ho[hb], hT_all[:, :, T * 512:(T + 1) * 512], hsb[hb][:], deps=evs[-1:])
                    if T == 0:
                        first_od = od
                    pend = (T, hb, evs[-1:], od)
                def load_h(it):
                    tt = it % 16
                    htoks[it] = dma("sp", sem_hh[it % 2], hsb[it % 2][:], hT_all[:, :, tt * 512:(tt + 1) * 512],
                                    deps=h_free[it % 2])

                pT_, phb, pev, pod = pend
                h_free[0] = h_free[0] + [first_od]
                load_h(16)
                lastmm = mm_group(0, pT_, hsb[phb], pev)
                h_free[phb] = [lastmm, pod]
                w_free[0] = [lastmm]
            pass0_done = barrier()
            h_free[1] = list(pass0_done)
            for wb in range(1, 8):
                if wb + 1 < 8:
                    load_w(wb + 1)
                for tt in range(16):
                    it = wb * 16 + tt
                    if it + 1 < 128:
                        load_h(it + 1)
                    lastmm = mm_group(wb, tt, hsb[it % 2], [htoks[it]])
                    h_free[it % 2] = [lastmm]
                w_free[wb % 2] = [lastmm]
            km_tok = op("dve", lambda e: e.tensor_scalar(out=kmean[:], in0=ksum[:], scalar1=1.0 / 256.0,
                                                         scalar2=None, op0=ALU.mult),
                        deps=barrier())

        prev_stage_toks = barrier()
        for hf in range(2):
            hs = ExitStack()
            with hs:
                def sbh(name, shape, dt):
                    return hs.enter_context(nc.sbuf_tensor(name, list(shape), dt))
                ypool = sbh("ypool%d" % hf, [128, 16, 1024], BF16)
                s2 = ExitStack()
                with s2:
                    def sb2(name, shape, dt):
                        return s2.enter_context(nc.sbuf_tensor(name, list(shape), dt))
                    prev_stage_toks = barrier()
                    hTo = sb2("hTo%d" % hf, [128, 32, 1152], BF16)
                    a_s = ExitStack()
                    with a_s:
                        nt = NT(a_s, "b%d" % hf, g1rep, prev_stage_toks)
                        hTo_toks = []
                        nt.p1(0, xo[hf, 0:128, :])
                        for t in range(9):
                            if t + 1 < 9:
                                nt.p1(t + 1, xo[hf, (t + 1) * 128:(t + 2) * 128, :])

                            def evac(g8, bank, tp, t=t):
                                dst = hTo[:, g8 * 4:(g8 + 1) * 4, t * 128:(t + 1) * 128]
                                ev = op("act", lambda e: e.copy(out=dst, in_=bank), deps=[tp] + list(prev_stage_toks))
                                hTo_toks.append(ev)
                                return [ev]
                            nt.p2(t, evac)
                        hTo_ready = hTo_toks[-1:]
                    s2a_done = barrier()
                    prev_stage_toks = s2a_done
                    w2 = [sb2("w2_%d_%d" % (hf, i), [128, 32, 256], BF16) for i in range(2)]
                    SG = sb2("SG%d" % hf, [128, 4, 1024], BF16)
                    diffT = sb2("diffT%d" % hf, [128, 4, 1024], BF16)
                    U = [sb2("U%d_%d" % (hf, i), [128, 528], F32) for i in range(2)]
                    TA = [sb2("TA%d_%d" % (hf, i), [128, 528], F32) for i in range(2)]
                    tmp16 = sb2("tmp16_%d" % hf, [128, 16], F32)
                    stg = [sb2("stg%d_%d" % (hf, i), [128, 512], BF16) for i in range(4)]
                    gwsb = sb2("gwsb%d" % hf, [128, 4, 4, 512], BF16)
                    sem_w2 = [mksem("s2w_%d" % i) for i in range(2)]
                    sem_stg = [mksem("s2s_%d" % i) for i in range(4)]
                    sem_gw = mksem("s2gw")
                    blocks = []
                    for g in range(4):
                        blocks.append(("gp", 2048 + 512 * g, 0, g))
                        blocks.append(("gp", 2048 + 512 * g + 256, 2, g))
                        blocks.append(("u", 512 * g, 0, g))
                        blocks.append(("u", 512 * g + 256, 2, g))
                    for h2 in range(8):
                        blocks.append(("q", 4096 + 256 * h2, 0, h2))
                        blocks.append(("ga", 10240 + 256 * h2, 0, h2))
                    for f2 in range(16):
                        blocks.append(("mp", 12288 + 256 * f2, 0, f2))
                    for f2 in range(16):
                        blocks.append(("ma", 16384 + 256 * f2, 0, f2))
                    w_free = [list(s2a_done), list(s2a_done)]
                    wtoks = {}

                    def load_w2(bi):
                        col0 = blocks[bi][1]
                        wtoks[bi] = dma("pool", sem_w2[bi % 2], w2[bi % 2][:], w_in_v[:, :, col0:col0 + 256],
                                        deps=w_free[bi % 2])
                    bank_free = [list(prev_stage_toks) for _ in range(6)]
                    stg_free = [list(s2a_done) for _ in range(4)]
                    U_free = [list(s2a_done), list(s2a_done)]
                    TA_free = [list(s2a_done), list(s2a_done)]
                    sg_toks = [None] * 8
                    diff_toks = {}
                    diff_free = list(s2a_done)
                    sg_free = list(s2a_done)
                    nb = 0
                    nstg = 0
                    nU = 0
                    s2_out = {}
                    yp_toks = []
                    load_w2(0)
                    load_w2(1)
                    gwtok = dma("pool", sem_gw, gwsb[:], gw_v, deps=s2a_done)
                    for bi, (typ, col0, cidx0, meta) in enumerate(blocks):
                        if bi >= 1 and bi + 1 < len(blocks):
                            load_w2(bi + 1)
                        W = w2[bi % 2]
                        lastmm = None
                        for sbi in range(2):
                            t0 = sbi * 528 + 16
                            for c in range(2):
                                bk = nb % 4
                                nb += 1
                                bank = ps[bk]
                                for kc in range(32):
                                    deps = [wtoks[bi]] + hTo_ready + bank_free[bk] if kc == 0 else ()
                                    mm = op("pe", lambda e, kc=kc, c=c: e.matmul(
                                        bank[:, :], lhsT=W[:, kc, c * 128:(c + 1) * 128], rhs=hTo[:, kc, t0:t0 + 512],
                                        start=(kc == 0), stop=(kc == 31)), deps=deps, signal=(kc == 31))
                                lastmm = mm
                                cg = cidx0 + c
                                tsl = slice(sbi * 512, (sbi + 1) * 512)
                                if typ == "gp":
                                    e1 = op("act", lambda e: e.activation(out=SG[:, cg, tsl], in_=bank[:, :], func=AF.Silu),
                                            deps=[mm] + sg_free)
                                    sg_toks[cg * 2 + sbi] = e1
                                    bank_free[bk] = [e1]
                                elif typ == "u":
                                    g = meta
                                    for kc in range(32):
                                        deps = bank_free[4] if kc == 0 else ()
                                        mh = op("pe", lambda e, kc=kc, c=c: e.matmul(
                                            ps[4][:, 0:16], lhsT=W[:, kc, c * 128:(c + 1) * 128],
                                            rhs=hTo[:, kc, sbi * 528:sbi * 528 + 16],
                                            start=(kc == 0), stop=(kc == 31)), deps=deps, signal=(kc == 31))
                                    lastmm = mh
                                    ub = nU % 2
                                    nU += 1
                                    Ub = U[ub]
                                    e1 = op("act", lambda e: e.copy(out=Ub[:, 16:528], in_=bank[:, :]),
                                            deps=[mm] + U_free[ub])
                                    e2 = op("act", lambda e: e.copy(out=Ub[:, 0:16], in_=ps[4][:, 0:16]), deps=[mh])
                                    bank_free[bk] = [e1]
                                    bank_free[4] = [e2]
                                    src = Ub
                                    dep = [e1, e2]
                                    ta_used = []
                                    for st in range(g + 1):
                                        sh = 1 << st
                                        lo = (1 << (st + 1)) - 1
                                        dstb = TA[st % 2]
                                        dd = op("dve", lambda e, src=src, dstb=dstb, lo=lo, sh=sh: e.tensor_tensor(
                                            out=dstb[:, lo:528], in0=src[:, lo:528], in1=src[:, lo - sh:528 - sh],
                                            op=ALU.add), deps=dep + TA_free[st % 2])
                                        dep = [dd]
                                        src = dstb
                                    wsum = src
                                    w = WINS[g]
                                    dz = op("dve", lambda e: e.scalar_tensor_tensor(
                                        out=diffT[:, cg, tsl], in0=wsum[:, 16:528], scalar=1.0 / w, in1=Ub[:, 16:528],
                                        op0=ALU.mult, op1=ALU.subtract), deps=dep + diff_free)
                                    last = dz
                                    if hf == 0 and sbi == 0:
                                        f1 = op("dve", lambda e: e.tensor_tensor(
                                            out=tmp16[:], in0=wsum[:, 16:32], in1=rc0[:, g, :], op=ALU.mult),
                                            deps=[dz, ctok_sp])
                                        f2_ = op("dve", lambda e: e.tensor_tensor(
                                            out=diffT[:, cg, 0:16], in0=tmp16[:], in1=Ub[:, 16:32], op=ALU.subtract),
                                            deps=[f1])
                                        last = f2_
                                    TA_free[0] = [last]
                                    TA_free[1] = [last]
                                    U_free[ub] = [last]
                                    diff_toks[(cg, sbi)] = last
                                    if cidx0 == 2 and c == 1:
                                        pm_last = None
                                        for oc in range(4):
                                            for kc4 in range(4):
                                                deps = ([diff_toks[(k, sbi)] for k in range(4)] + bank_free[5] + [gwtok]) if kc4 == 0 else ()
                                                pm = op("pe", lambda e, oc=oc, kc4=kc4: e.matmul(
                                                    ps[5][:, :], lhsT=gwsb[:, g, kc4, oc * 128:(oc + 1) * 128],
                                                    rhs=diffT[:, kc4, tsl], start=(kc4 == 0), stop=(kc4 == 3)),
                                                    deps=deps, signal=(kc4 == 3))
                                            yy = op("dve", lambda e, oc=oc: e.scalar_tensor_tensor(
                                                out=ypool[:, g * 4 + oc, tsl], in0=ps[5][:, :],
                                                scalar=pscale_sb[:, g * 4 + oc:g * 4 + oc + 1], in1=SG[:, oc, tsl],
                                                op0=ALU.mult, op1=ALU.mult),
                                                deps=[pm, sg_toks[oc * 2 + sbi], ctok_sp])
                                            bank_free[5] = [yy]
                                            yp_toks.append(yy)
                                            pm_last = pm
                                        lastmm = pm_last
                                        if sbi == 1:
                                            diff_free = [pm_last]
                                            sg_free = [yy]
                                else:
                                    sl = nstg % 4
                                    nstg += 1
                                    if typ == "q":
                                        e1 = op("act", lambda e: e.activation(out=stg[sl][:], in_=bank[:, :], func=AF.Copy,
                                                                               scale=float(128.0 ** -0.5)),
                                                deps=[mm] + stg_free[sl])
                                        dst = qT_s[:, meta * 2 + c, tsl]
                                    elif typ == "ga":
                                        e1 = op("act", lambda e: e.activation(out=stg[sl][:], in_=bank[:, :], func=AF.Silu),
                                                deps=[mm] + stg_free[sl])
                                        dst = sga_s[:, meta * 2 + c, tsl]
                                    else:
                                        e1 = op("act", lambda e: e.activation(out=stg[sl][:], in_=bank[:, :], func=AF.Sigmoid),
                                                deps=[mm] + stg_free[sl])
                                        dst = (smp_s if typ == "mp" else sma_s)[:, meta * 2 + c, tsl]
                                    bank_free[bk] = [e1]
                                    od = dma("sp", sem_stg[sl], dst, stg[sl][:], deps=[e1])
                                    stg_free[sl] = [od]
                                    s2_out[od[0]] = od
                        w_free[bi % 2] = [lastmm]
                prev_stage_toks = barrier()

                yattn = sbh("yattn%d" % hf, [128, 16, 1024], BF16)
                sB = ExitStack()
                with sB:
                    def sbB(name, shape, dt):
                        return sB.enter_context(nc.sbuf_tensor(name, list(shape), dt))
                    NBLK = 16 if hf == 0 else 32
                    KT = [sbB("KT%d_%d" % (hf, i), [128, NBLK * 256], BF16) for i in range(2)]
                    VV = [sbB("VV%d_%d" % (hf, i), [128, NBLK * 2, 128], BF16) for i in range(2)]
                    QT = [sbB("QT%d_%d" % (hf, i), [128, 1024], BF16) for i in range(2)]
                    GA = [sbB("GA%d_%d" % (hf, i), [128, 1024], BF16) for i in range(2)]
                    PT = [sbB("PT%d_%d" % (hf, i), [128, 512], BF16) for i in range(3)]
                    onehot = sbB("onehot%d" % hf, [128, 4096], BF16)
                    cmask = sbB("cmask%d" % hf, [128, 16, 512], BF16)
                    gm = sbB("gm%d" % hf, [128, 32], F32)
                    m8 = sbB("m8_%d" % hf, [128, 8], F32)
                    t2b = sbB("t2b%d" % hf, [128, 32], F32)
                    selb = [sbB("selb%d_%d" % (hf, i), [128, 4, 32], BF16) for i in range(2)]
                    selT = [sbB("selT%d_%d" % (hf, i), [128, 4, 128], BF16) for i in range(2)]
                    zt0 = op("dve", lambda e: e.memset(selT[0][:], 0.0), deps=prev_stage_toks)
                    zt1 = op("dve", lambda e: e.memset(selT[1][:], 0.0), deps=prev_stage_toks)
                    rden = sbB("rden%d" % hf, [128, 512], F32)
                    onum = sbB("onum%d" % hf, [128, 512], F32)
                    dacc = sbB("dacc%d" % hf, [128, 512], F32)
                    dacc2 = [dacc, sbB("daccb%d" % hf, [128, 512], F32)]
                    PTx = sbB("PTx%d" % hf, [128, 512], BF16)
                    ptsum = sbB("ptsum%d" % hf, [128, 512], BF16)
                    ones32 = sbB("ones32_%d" % hf, [128, 128], F32)
                    o32tok = dma("sp", mksem("Bo32"), ones32[:], ones_in, deps=prev_stage_toks)
                    dacc_free = list(prev_stage_toks) + [o32tok]
                    sem_kv = [mksem("Bkv_%d" % i) for i in range(2)]
                    sem_q = [mksem("Bq_%d" % i) for i in range(2)]
                    sem_cm = mksem("Bcm")
                    cm1 = dma("pool", sem_cm, onehot[:], onehot_in, deps=prev_stage_toks)
                    cm2 = dma("pool", sem_cm, cmask[:], cmask_in)
                    cmtok = cm2
                    kv_free = [list(prev_stage_toks), list(prev_stage_toks)]
                    q_free = [list(prev_stage_toks), list(prev_stage_toks)]
                    kvtoks = {}
                    qtoks = {}

                    def load_head(h):
                        s_ = h % 2
                        dma("sp", sem_kv[s_], KT[s_][:], kT_all[:, h, 0:NBLK * 256], deps=kv_free[s_])
                        kvtoks[h] = dma("sp", sem_kv[s_], VV[s_][:], v_all_v[:, 0:NBLK * 2, h * 128:(h + 1) * 128])
                        dma("sp", sem_q[s_], QT[s_][:], qT_s[:, h, :], deps=q_free[s_])
                        qtoks[h] = dma("sp", sem_q[s_], GA[s_][:], sga_s[:, h, :])

                    S_free = [list(prev_stage_toks) for _ in range(3)]
                    PT_free = [list(prev_stage_toks) for _ in range(3)]
                    OD_free = list(prev_stage_toks)
                    gate_free = list(prev_stage_toks)
                    pbB_free = list(prev_stage_toks)
                    ya_toks = []
                    load_head(0)
                    gidx = 0
                    iters = [(h, k) for h in range(16) for k in range(2)]
                    stk_tok = {}
                    gchain = {}

                    def hdeps(h):
                        return [kvtoks[h], qtoks[h], cmtok, ctok_pool]

                    def gate_p1(j):
                        nonlocal gate_free
                        h, k = iters[j]
                        slot = 2 * hf + k
                        Q_ = QT[h % 2]
                        sbuf_ = selb[j % 2]
                        last = None
                        gmms = []
                        for qt in range(4):
                            gmms.append(op("pe", lambda e, qt=qt: e.matmul(
                                ps[4][:, qt * 32:(qt + 1) * 32], lhsT=Q_[:, k * 512 + qt * 128:k * 512 + (qt + 1) * 128],
                                rhs=kmean[:, h * 32:(h + 1) * 32], start=True, stop=True),
                                deps=hdeps(h) + gate_free + den_free))
                        for qt in range(4):
                            mi = slot * 2 + qt // 2
                            gmm = gmms[qt]
                            g1_ = op("dve", lambda e, qt=qt: e.tensor_tensor(out=gm[:], in0=ps[4][:, qt * 32:(qt + 1) * 32],
                                                                              in1=pastmask[:, mi, :], op=ALU.add),
                                     deps=[gmm, ctok_sp])
                            if qt == 3:
                                gate_free = [g1_]
                            g2_ = op("dve", lambda e: e.max(out=m8[:], in_=gm[:]), deps=[g1_])
                            g3_ = op("dve", lambda e: e.scalar_tensor_tensor(
                                out=t2b[:], in0=gm[:], scalar=m8[:, 2:3], in1=pastvalid[:, mi, :],
                                op0=ALU.is_ge, op1=ALU.mult), deps=[g2_])
                            last = op("dve", lambda e, qt=qt: e.scalar_tensor_tensor(
                                out=sbuf_[:, qt, :], in0=t2b[:], scalar=-NEG, in1=ownbias[:, mi, :],
                                op0=ALU.mult, op1=ALU.add), deps=[g3_] + selb_free[j % 2])
                        gchain[j] = last

                    def gate_p2(j):
                        nonlocal pbB_free
                        sbuf_ = selb[j % 2]
                        tp = None
                        for qt in range(4):
                            tp = op("pe", lambda e, qt=qt: e.transpose(
                                out=pb[0][0:32, qt, :], in_=sbuf_[:, qt, :], identity=ident[:]),
                                deps=[gchain[j]] + pbB_free, signal=(qt == 3))
                        selb_free[j % 2] = [tp]
                        stk = op("dve", lambda e: e.tensor_copy(out=selT[j % 2][0:32, :, :], in_=pb[0][0:32, 0:4, :]),
                                 deps=[tp] + selT_free[j % 2])
                        pbB_free = [stk]
                        stk_tok[j] = stk

                    selb_free = [list(prev_stage_toks), list(prev_stage_toks)]
                    selT_free = [[zt0], [zt1]]
                    den_free = list(prev_stage_toks)
                    OB = [ps[3], ps[5]]
                    OB_free = [list(prev_stage_toks), list(prev_stage_toks)]
                    dacc_free2 = [list(dacc_free), list(dacc_free)]
                    PT4 = PT + [PTx]
                    PT_free = [list(prev_stage_toks) for _ in range(4)]
                    gate_p1(0)
                    gate_p2(0)

                    class It:
                        pass
                    its = []
                    for j, (h, k) in enumerate(iters):
                        it_ = It()
                        it_.j, it_.h, it_.k = j, h, k
                        it_.slot = 2 * hf + k
                        it_.qsl = slice(k * 512, (k + 1) * 512)
                        it_.chunks = [(n, c) for n in range(8 * it_.slot + 8) for c in range(2)]
                        its.append(it_)
                    items = [(it_, idx) for it_ in its for idx in range(len(it_.chunks))]
                    s_tok = {}

                    def emit_S(g):
                        it_, idx = items[g]
                        h, slot = it_.h, it_.slot
                        K_, Q_ = KT[h % 2], QT[h % 2]
                        n, c = it_.chunks[idx]
                        bk = g % 3
                        cc = 2 * n + c
                        diag = n >= 8 * slot
                        op("pe", lambda e: e.matmul(ps[bk][:, :], lhsT=K_[:, cc * 128:(cc + 1) * 128],
                                                    rhs=Q_[:, it_.qsl], start=True, stop=False),
                           deps=hdeps(h) + S_free[bk], signal=False)
                        t_ = op("pe", lambda e: e.matmul(ps[bk][:, :], lhsT=onehot[:, n * 128:(n + 1) * 128],
                                                         rhs=selT[it_.j % 2][:, :, :], start=False, stop=(not diag)),
                                deps=[stk_tok[it_.j]], signal=(not diag))
                        if diag:
                            m = n - 8 * slot
                            t_ = op("pe", lambda e: e.matmul(ps[bk][:, :], lhsT=ident[:, :],
                                                             rhs=cmask[:, m * 2 + c, :], start=False, stop=True))
                        s_tok[g] = t_

                    fin = {}

                    def finalize(it_):
                        nonlocal den_free
                        j, h = it_.j, it_.h
                        G_ = GA[h % 2]
                        dm = op("pe", lambda e: e.matmul(ps[4][:, :], lhsT=ones32[:, :], rhs=dacc2[j % 2][:], start=True, stop=True),
                                deps=[it_.ac] + den_free + gate_free)
                        dacc_free2[j % 2] = [dm]
                        f1 = op("dve", lambda e: e.reciprocal(out=rden[:], in_=ps[4][:, :]), deps=[dm])
                        den_free = [f1]
                        f2 = op("dve", lambda e: e.tensor_tensor(out=onum[:], in0=OB[j % 2][:, :], in1=rden[:], op=ALU.mult),
                                deps=[f1, it_.pv])
                        OB_free[j % 2] = [f2]
                        f3 = op("dve", lambda e: e.tensor_tensor(out=yattn[:, h, it_.qsl], in0=onum[:], in1=G_[:, it_.qsl],
                                                                 op=ALU.mult), deps=[f2])
                        if it_.k == 1:
                            kv_free[h % 2] = [it_.pv]
                            q_free[h % 2] = [it_.pv, f3]

                    emit_S(0)
                    emit_S(1)
                    for g, (it_, idx) in enumerate(items):
                        j, h, k = it_.j, it_.h, it_.k
                        NCH = len(it_.chunks)
                        V_ = VV[h % 2]
                        if g + 2 < len(items):
                            emit_S(g + 2)
                        if idx == 0 and j + 1 < len(its):
                            gate_p1(j + 1)
                        if idx == 3 and j > 0:
                            finalize(its[j - 1])
                        if idx == 4 and k == 0 and h + 1 < 16:
                            load_head(h + 1)
                        if idx == 7 and j + 1 < len(its):
                            gate_p2(j + 1)
                        n, c = it_.chunks[idx]
                        cc = 2 * n + c
                        bk = g % 3
                        pi = g % 4
                        ex = op("act", lambda e: e.activation(out=PT4[pi][:], in_=ps[bk][:, :], func=AF.Exp),
                                deps=[s_tok.pop(g)] + PT_free[pi])
                        S_free[bk] = [ex]
                        pv = op("pe", lambda e: e.matmul(OB[j % 2][:, :], lhsT=V_[:, cc, :], rhs=PT4[pi][:],
                                                         start=(idx == 0), stop=(idx == NCH - 1)),
                                deps=[ex] + (OB_free[j % 2] if idx == 0 else []))
                        if idx % 2 == 0:
                            it_.ex_prev = ex
                            it_.pv_prev = pv
                        else:
                            pp = (g - 1) % 4
                            pa = op("dve", lambda e: e.tensor_tensor(out=ptsum[:], in0=PT4[pp][:], in1=PT4[pi][:], op=ALU.add),
                                    deps=[it_.ex_prev, ex])
                            if idx == 1:
                                ac = op("dve", lambda e: e.tensor_copy(out=dacc2[j % 2][:], in_=ptsum[:]),
                                        deps=[pa] + dacc_free2[j % 2])
                            else:
                                ac = op("dve", lambda e: e.tensor_tensor(out=dacc2[j % 2][:], in0=dacc2[j % 2][:], in1=ptsum[:],
                                                                         op=ALU.add), deps=[pa, it_.ac])
                            PT_free[pp] = [it_.pv_prev, pa]
                            PT_free[pi] = [pv, pa]
                            it_.ac = ac
                        it_.pv = pv
                        if idx == NCH - 1:
                            selT_free[j % 2] = [pv]
                    finalize(its[-1])
                prev_stage_toks = barrier()

                mergedT = sbh("mergedT%d" % hf, [128, 32, 1024], BF16)
                sC = ExitStack()
                with sC:
                    def sbC(name, shape, dt):
                        return sC.enter_context(nc.sbuf_tensor(name, list(shape), dt))
                    wP = [sbC("wP%d_%d" % (hf, i), [128, 16, 256], BF16) for i in range(2)]
                    wA = [sbC("wA%d_%d" % (hf, i), [128, 16, 256], BF16) for i in range(2)]
                    gP = [sbC("gP%d_%d" % (hf, i), [128, 2, 1024], BF16) for i in range(2)]
                    gA = [sbC("gA%d_%d" % (hf, i), [128, 2, 1024], BF16) for i in range(2)]
                    t1s = [sbC("t1s%d_%d" % (hf, i), [128, 512], F32) for i in range(2)]
                    sem_wc = [mksem("Cw_%d" % i) for i in range(2)]
                    sem_gc = [mksem("Cg_%d" % i) for i in range(2)]
                    wc_free = [list(prev_stage_toks), list(prev_stage_toks)]
                    gc_free = [list(prev_stage_toks), list(prev_stage_toks)]
                    wct = {}
                    gct = {}

                    def load_c(bi):
                        s_ = bi % 2
                        dma("pool", sem_wc[s_], wP[s_][:], wpo_v[:, :, bi * 256:(bi + 1) * 256], deps=wc_free[s_])
                        wct[bi] = dma("pool", sem_wc[s_], wA[s_][:], wao_v[:, :, bi * 256:(bi + 1) * 256])
                        dma("sp", sem_gc[s_], gP[s_][:], smp_s[:, bi * 2:bi * 2 + 2, :], deps=gc_free[s_])
                        gct[bi] = dma("sp", sem_gc[s_], gA[s_][:], sma_s[:, bi * 2:bi * 2 + 2, :])
                    bank_free = [list(prev_stage_toks) for _ in range(4)]
                    t1_free = [list(prev_stage_toks), list(prev_stage_toks)]
                    load_c(0)
                    ncc = 0
                    mg_toks = []
                    for bi in range(16):
                        if bi + 1 < 16:
                            load_c(bi + 1)
                        s_ = bi % 2
                        lastmm = None
                        lastd = None
                        for sbi in range(2):
                            tsl = slice(sbi * 512, (sbi + 1) * 512)
                            for c in range(2):
                                f = bi * 2 + c
                                b1 = (ncc % 2) * 2
                                b2 = b1 + 1
                                ti = ncc % 2
                                ncc += 1
                                for kc in range(16):
                                    deps = [wct[bi]] + bank_free[b1] if kc == 0 else ()
                                    m1 = op("pe", lambda e, kc=kc: e.matmul(
                                        ps[b1][:, :], lhsT=wP[s_][:, kc, c * 128:(c + 1) * 128], rhs=ypool[:, kc, tsl],
                                        start=(kc == 0), stop=(kc == 15)), deps=deps, signal=(kc == 15))
                                for kc in range(16):
                                    deps = bank_free[b2] if kc == 0 else ()
                                    m2 = op("pe", lambda e, kc=kc: e.matmul(
                                        ps[b2][:, :], lhsT=wA[s_][:, kc, c * 128:(c + 1) * 128], rhs=yattn[:, kc, tsl],
                                        start=(kc == 0), stop=(kc == 15)), deps=deps, signal=(kc == 15))
                                lastmm = m2
                                d1 = op("dve", lambda e: e.tensor_tensor(out=t1s[ti][:], in0=ps[b1][:, :], in1=gP[s_][:, c, tsl],
                                                                         op=ALU.mult), deps=[m1, gct[bi]] + t1_free[ti])
                                d2 = op("dve", lambda e: e.tensor_tensor(out=mergedT[:, f, tsl], in0=ps[b2][:, :],
                                                                         in1=gA[s_][:, c, tsl], op=ALU.mult), deps=[m2])
                                d3 = op("dve", lambda e: e.tensor_tensor(out=mergedT[:, f, tsl], in0=mergedT[:, f, tsl],
                                                                         in1=t1s[ti][:], op=ALU.add), deps=[d2, d1])
                                bank_free[b1] = [d1]
                                bank_free[b2] = [d2]
                                t1_free[ti] = [d3]
                                lastd = d3
                        wc_free[s_] = [lastmm]
                        gc_free[s_] = [lastd]
                        mg_toks.append(lastd)
                prev_stage_toks = barrier()

                sD = ExitStack()
                with sD:
                    def sbD(name, shape, dt):
                        return sD.enter_context(nc.sbuf_tensor(name, list(shape), dt))
                    wD = [sbD("wD%d_%d" % (hf, i), [128, 32, 256], BF16) for i in range(2)]
                    ost = [sbD("ost%d_%d" % (hf, i), [128, 256], F32) for i in range(4)]
                    sem_wd = [mksem("Dw_%d" % i) for i in range(2)]
                    sem_os = [mksem("Do_%d" % i) for i in range(4)]
                    wd_free = [list(prev_stage_toks), list(prev_stage_toks)]
                    wdt = {}

                    def load_d(bi):
                        wdt[bi] = dma("pool", sem_wd[bi % 2], wD[bi % 2][:], wo_v[:, :, bi * 256:(bi + 1) * 256],
                                      deps=wd_free[bi % 2])
                    bank_free = [list(prev_stage_toks) for _ in range(4)]
                    ost_free = [list(prev_stage_toks) for _ in range(4)]
                    load_d(0)
                    nd = 0
                    for bi in range(16):
                        if bi + 1 < 16:
                            load_d(bi + 1)
                        W = wD[bi % 2]
                        for tile in range(8):
                            bk = nd % 4
                            sl = nd % 4
                            nd += 1
                            for kc in range(32):
                                deps = [wdt[bi]] + bank_free[bk] if kc == 0 else ()
                                mm = op("pe", lambda e, kc=kc: e.matmul(
                                    ps[bk][:, 0:256], lhsT=mergedT[:, kc, tile * 128:(tile + 1) * 128], rhs=W[:, kc, :],
                                    start=(kc == 0), stop=(kc == 31)), deps=deps, signal=(kc == 31))
                            if nd % 2 == 0:
                                d1 = op("dve", lambda e: e.tensor_copy(out=ost[sl][:], in_=ps[bk][:, 0:256]),
                                        deps=[mm] + ost_free[sl])
                            else:
                                d1 = op("act", lambda e: e.copy(out=ost[sl][:], in_=ps[bk][:, 0:256]),
                                        deps=[mm] + ost_free[sl])
                            bank_free[bk] = [d1]
                            od = dma("sp", sem_os[sl], O_s[tile, :, bi * 256:(bi + 1) * 256], ost[sl][:], deps=[d1])
                            ost_free[sl] = [od]
                        wd_free[bi % 2] = [mm]
                prev_stage_toks = barrier()
            hs2 = ExitStack()
            with hs2:
                def sbe(name, shape, dt):
                    return hs2.enter_context(nc.sbuf_tensor(name, list(shape), dt))
                x1T = sbe("x1T%d" % hf, [128, 32, 1024], BF16)
                sE1 = ExitStack()
                with sE1:
                    def sbE(name, shape, dt):
                        return sE1.enter_context(nc.sbuf_tensor(name, list(shape), dt))
                    g2 = sbE("g2_%d" % hf, [128, D], F32)
                    Ob = [sbE("Ob%d_%d" % (hf, i), [128, D], F32) for i in range(3)]
                    Xb = [sbE("Xb%d_%d" % (hf, i), [128, D], F32) for i in range(3)]
                    x1b = [sbE("x1b%d_%d" % (hf, i), [128, D], BF16) for i in range(2)]
                    rs = sbE("rs%d" % hf, [128, 8], F32)
                    rs2 = sbE("rs2_%d" % hf, [128, 8], F32)
                    sem_o = [mksem("Eo_%d" % i) for i in range(3)]
                    sem_xx = [mksem("Ex_%d" % i) for i in range(3)]
                    sem_x1 = [mksem("Ex1_%d" % i) for i in range(3)]
                    sem_g2 = mksem("Eg2_")
                    g2tok = dma("sp", sem_g2, g2[:], g2rep, deps=prev_stage_toks)
                    o_free = [list(prev_stage_toks) for _ in range(3)]
                    x_free = [list(prev_stage_toks) for _ in range(3)]
                    x1b_free = [list(prev_stage_toks), list(prev_stage_toks)]
                    pb_free = [list(prev_stage_toks), list(prev_stage_toks)]
                    otok = {}
                    xtok = {}

                    def load_e1(tile):
                        b = tile % 3
                        sbi, tq = tile // 4, tile % 4
                        r0 = sbi * 528 + 16 + tq * 128
                        otok[tile] = dma("sp", sem_o[b], Ob[b][:], O_s[tile, :, :], deps=o_free[b])
                        xtok[tile] = dma("sp", sem_xx[b], Xb[b][:], xo[hf, r0:r0 + 128, :], deps=x_free[b])
                    junk = sbE("junkE%d" % hf, [128, D], BF16)
                    x1T_toks = []
                    a3s = {}
                    d3s = {}

                    def e1_p1(tile):
                        b = tile % 3
                        a1 = op("act", lambda e: e.activation(out=junk[:], in_=Ob[b][:], func=AF.Square,
                                                               accum_out=rs[:, tile:tile + 1]),
                                deps=[otok[tile]] + list(prev_stage_toks))
                        a2 = op("act", lambda e: e.activation(out=rs[:, tile:tile + 1], in_=rs[:, tile:tile + 1],
                                                               func=AF.Sqrt, scale=1.0 / D, bias=eps_t[:, 0:1]), deps=[a1])
                        d1 = op("dve", lambda e: e.reciprocal(out=rs2[:, tile:tile + 1], in_=rs[:, tile:tile + 1]), deps=[a2])
                        d2 = op("dve", lambda e: e.scalar_tensor_tensor(out=Ob[b][:], in0=Ob[b][:], scalar=rs2[:, tile:tile + 1],
                                                                        in1=g2[:], op0=ALU.mult, op1=ALU.mult),
                                deps=[d1, g2tok, a1])
                        d3 = op("dve", lambda e: e.tensor_tensor(out=Xb[b][:], in0=Xb[b][:], in1=Ob[b][:], op=ALU.add),
                                deps=[d2, xtok[tile]])
                        o_free[b] = [d3]
                        od = dma("sp", sem_x1[b], x1_s[tile, :, :], Xb[b][:], deps=[d3])
                        d3s[tile] = (d3, od)

                    def e1_p1b(tile):
                        b = tile % 3
                        b2 = tile % 2
                        d3, od = d3s.pop(tile)
                        a3 = op("act", lambda e: e.copy(out=x1b[b2][:], in_=Xb[b][:]), deps=[d3] + x1b_free[b2])
                        x_free[b] = [od, a3]
                        a3s[tile] = a3

                    def e1_p2(tile):
                        b = tile % 2
                        a3 = a3s.pop(tile)
                        tp = None
                        for g8 in range(8):
                            bank = pb[g8 % 2]
                            for j in range(4):
                                kc = g8 * 4 + j
                                tp = op("pe", lambda e, kc=kc, j=j: e.transpose(
                                    out=bank[:, j, :], in_=x1b[b][:, kc * 128:(kc + 1) * 128], identity=ident[:]),
                                    deps=[a3] + pb_free[g8 % 2], signal=(j == 3))
                            dst = x1T[:, g8 * 4:(g8 + 1) * 4, tile * 128:(tile + 1) * 128]
                            ev = op("act", lambda e: e.copy(out=dst, in_=bank[:, 0:4, :]), deps=[tp] + list(prev_stage_toks))
                            pb_free[g8 % 2] = [ev]
                            x1T_toks.append(ev)
                        x1b_free[b] = [tp]

                    load_e1(0)
                    load_e1(1)
                    load_e1(2)
                    e1_p1(0)
                    e1_p1b(0)
                    for tile in range(8):
                        if tile + 1 < 8:
                            e1_p1(tile + 1)
                        e1_p2(tile)
                        if tile + 1 < 8:
                            e1_p1b(tile + 1)
                        if tile + 3 < 8:
                            load_e1(tile + 3)
                prev_stage_toks = barrier()

                sE2 = ExitStack()
                with sE2:
                    def sbF(name, shape, dt):
                        return sE2.enter_context(nc.sbuf_tensor(name, list(shape), dt))
                    wG = [sbF("wG%d_%d" % (hf, i), [128, 32, 512], BF16) for i in range(2)]
                    wPp = [sbF("wPp%d_%d" % (hf, i), [128, 2, 512], BF16) for i in range(2)]
                    pTs = sbF("pTs%d" % hf, [128, 2, 1024], BF16)
                    x1t = [sbF("x1t%d_%d" % (hf, i), [128, 512], F32) for i in range(3)]
                    sgt = [sbF("sgt%d_%d" % (hf, i), [128, 512], F32) for i in range(2)]
                    sem_wg = [mksem("Fw_%d" % i) for i in range(2)]
                    sem_x1t = [mksem("Fx_%d" % i) for i in range(3)]
                    sem_out = [mksem("Fo_%d" % i) for i in range(3)]
                    sem_pt = mksem("Fp")
                    wg_free = [list(prev_stage_toks), list(prev_stage_toks)]
                    wgt = {}

                    def load_g(bi):
                        s_ = bi % 2
                        dma("pool", sem_wg[s_], wG[s_][:], wg_v[:, :, bi * 512:(bi + 1) * 512], deps=wg_free[s_])
                        wgt[bi] = dma("pool", sem_wg[s_], wPp[s_][:], wp_v[:, :, bi * 512:(bi + 1) * 512])
                    x1t_free = [list(prev_stage_toks) for _ in range(3)]
                    x1tt = {}
                    items = [(bi, tile) for bi in range(8) for tile in range(8)]

                    def load_x1t(ii):
                        bi, tile = items[ii]
                        s_ = ii % 3
                        x1tt[ii] = dma("sp", sem_x1t[s_], x1t[s_][:], x1_s[tile, :, bi * 512:(bi + 1) * 512],
                                       deps=x1t_free[s_])
                    bank_free = [list(prev_stage_toks) for _ in range(4)]
                    sgt_free = [list(prev_stage_toks), list(prev_stage_toks)]
                    out_toks = {}
                    load_g(0)
                    pttok = dma("pool", sem_pt, pTs[:], pT_v[:, :, hf * 1024:(hf + 1) * 1024], deps=prev_stage_toks)
                    load_x1t(0)
                    for ii, (bi, tile) in enumerate(items):
                        if tile == 0 and bi + 1 < 8:
                            load_g(bi + 1)
                        s_ = bi % 2
                        bG = (ii % 2) * 2
                        bP = bG + 1
                        for kc in range(32):
                            deps = [wgt[bi]] + bank_free[bG] if kc == 0 else ()
                            mG = op("pe", lambda e, kc=kc: e.matmul(
                                ps[bG][:, :], lhsT=x1T[:, kc, tile * 128:(tile + 1) * 128], rhs=wG[s_][:, kc, :],
                                start=(kc == 0), stop=(kc == 31)), deps=deps, signal=(kc == 31))
                        for kc in range(2):
                            deps = [pttok] + bank_free[bP] if kc == 0 else ()
                            mP = op("pe", lambda e, kc=kc: e.matmul(
                                ps[bP][:, :], lhsT=pTs[:, kc, tile * 128:(tile + 1) * 128], rhs=wPp[s_][:, kc, :],
                                start=(kc == 0), stop=(kc == 1)), deps=deps, signal=(kc == 1))
                        si = ii % 2
                        xi = ii % 3
                        a1 = op("act", lambda e: e.activation(out=sgt[si][:], in_=ps[bG][:, :], func=AF.Sigmoid),
                                deps=[mG] + sgt_free[si])
                        d1 = op("dve", lambda e: e.tensor_tensor(out=sgt[si][:], in0=sgt[si][:], in1=ps[bP][:, :], op=ALU.mult),
                                deps=[a1, mP])
                        d2 = op("dve", lambda e: e.tensor_tensor(out=x1t[xi][:], in0=x1t[xi][:], in1=sgt[si][:], op=ALU.add),
                                deps=[d1, x1tt[ii]])
                        bank_free[bG] = [a1]
                        bank_free[bP] = [d1]
                        sgt_free[si] = [d2]
                        od = dma("sp", sem_out[xi], out_own[hf * 8 + tile, :, bi * 512:(bi + 1) * 512], x1t[xi][:], deps=[d2])
                        x1t_free[xi] = [od]
                        out_toks[od[0]] = od
                        if ii + 1 < len(items):
                            load_x1t(ii + 1)
                        if tile == 7:
                            wg_free[s_] = [mG, mP]
                prev_stage_toks = barrier()
        sc.wait("sp", barrier())
    return nc


_NC_CACHE = {}


def kernel(x, p, norm_pre, w_in, pool_group_w, pool_scale, w_pool_out, w_attn_out,
           w_out, norm_post, w_ple_proj, w_ple_gate):
    f32 = np.float32
    x = np.ascontiguousarray(np.asarray(x, f32))
    p = np.asarray(p, f32)
    w_in0 = np.ascontiguousarray(np.asarray(w_in, f32)[0])
    gw0 = np.ascontiguousarray(np.asarray(pool_group_w, f32)[0])
    wpo0 = np.ascontiguousarray(np.asarray(w_pool_out, f32)[0])
    wao0 = np.ascontiguousarray(np.asarray(w_attn_out, f32)[0])
    wo0 = np.ascontiguousarray(np.asarray(w_out, f32)[0])
    wg0 = np.ascontiguousarray(np.asarray(w_ple_gate, f32)[0])
    wp0 = np.ascontiguousarray(np.asarray(w_ple_proj, f32)[0])
    g1rep = np.ascontiguousarray(np.broadcast_to(np.asarray(norm_pre, f32)[0][None, :], (128, D)))
    g2rep = np.ascontiguousarray(np.broadcast_to(np.asarray(norm_post, f32)[0][None, :], (128, D)))
    pscale = np.ascontiguousarray(np.asarray(pool_scale, f32)[0].reshape(16, 128).T)
    ident = np.eye(128, dtype=f32)
    ones = np.ones((128, 128), f32)
    onehot = np.zeros((128, 32, 128), f32)
    for n in range(32):
        onehot[n, n, :] = 1.0
    onehot = onehot.reshape(128, 4096)

    if "nc" not in _NC_CACHE:
        _NC_CACHE["nc"] = build_program()
    nc = _NC_CACHE["nc"]

    in_maps = []
    own_tokens = []
    for c in range(8):
        b, r = c // 4, c % 4
        xo = np.zeros((2, 1152, D), f32)
        toks = []
        for i in range(4):
            hf, k = i // 2, i % 2
            s = 4 * i + r
            t0 = s * 512
            base = k * 528
            if t0 > 0:
                xo[hf, base:base + 16] = x[b, t0 - 16:t0]
            xo[hf, base + 16:base + 528] = x[b, t0:t0 + 512]
            toks.append(np.arange(t0, t0 + 512))
        toks = np.concatenate(toks)
        own_tokens.append(toks)
        pT = np.ascontiguousarray(p[0, b][toks, :].T)
        cmask = np.zeros((128, 16, 512), f32)
        pidx = np.arange(128)[:, None]
        qpos = (np.arange(512) % 256)[None, :]
        half = (np.arange(512) // 256)[None, :]
        for m in range(8):
            for cc in range(2):
                keyb = cc * 128 + pidx
                own = (2 * r + half) == m
                cmask[:, m * 2 + cc, :] = np.where(own & (keyb > qpos), NEG, 0.0)
        pastmask = np.zeros((128, 8, 32), f32)
        pastvalid = np.zeros((128, 8, 32), f32)
        ownbias = np.zeros((128, 8, 32), f32)
        nidx = np.arange(32)
        for slot in range(4):
            for qb in range(2):
                j = 2 * (4 * slot + r) + qb
                pastvalid[:, slot * 2 + qb, :] = (nidx < j).astype(f32)[None, :]
                pastmask[:, slot * 2 + qb, :] = np.where(nidx < j, 0.0, -1e30)[None, :]
                ownbias[:, slot * 2 + qb, :] = np.where(nidx == j, 0.0, NEG)[None, :]
        rc0 = np.zeros((128, 4, 16), f32)
        for g, w in enumerate(WINS):
            if r == 0:
                rc0[:, g, :] = (1.0 / np.minimum(np.arange(16) + 1, w)).astype(f32)[None, :]
            else:
                rc0[:, g, :] = 1.0 / w
        in_maps.append({
            "xs": x[b], "xo": xo, "pT": pT, "w_in": w_in0, "gw": gw0, "wpo": wpo0, "wao": wao0,
            "wo": wo0, "wg": wg0, "wp": wp0, "g1rep": g1rep, "g2rep": g2rep, "pscale": pscale,
            "ident": ident, "ones": ones, "onehot": onehot, "cmask": cmask, "pastmask": pastmask,
            "pastvalid": pastvalid, "ownbias": ownbias, "rc0": rc0,
        })
    res = run_bass_kernel_spmd(nc, in_maps, core_ids=list(range(8)))
    out = np.empty((2, S, D), f32)
    for c in range(8):
        b = c // 4
        o = np.asarray(res.results[c]["out_own"]).reshape(2048, D)
        out[b, own_tokens[c], :] = o
    if DEBUG_OUTS:
        kernel.debug = [{k: np.asarray(res.results[c][k]) for k in DEBUG_OUTS} for c in range(8)]
    return out
```
